# Optimizing a Trainium2 kernel written in Bass

```python
import jax, jax.numpy as jnp
from jax import lax
import numpy as np

D_MODEL = 1024
BATCH = 32
SEQ = 256
DEPTH = 1
DEC_BATCH = 2
DEC_SEQ = 1024
PAST_LEN = 512

GRID_W = 64
D_A = 512
N_HEADS_A = 4
HEAD_A = D_A // N_HEADS_A
N_DIRS = 2
D_B = 512
N_POOL_GROUPS = 4
POOL_GROUP = D_B // N_POOL_GROUPS
POOL_WINDOWS = (2, 4, 8, 16)
D_FF = -(-8 * D_MODEL // (3 * 256)) * 256
CHUNK = 32
EPS = 1e-6
SPLITS = [D_A, 2 * D_A, 3 * D_A, 4 * D_A, 5 * D_A, 5 * D_A + D_B, 5 * D_A + D_B + D_MODEL]
IN_COLS = 5 * D_A + D_B + 2 * D_MODEL

kernel_name = 'hybrid_hgrn2_pool_prefix_dit_step'


def rmsnorm(x, g):
    x32 = x.astype(jnp.float32)
    y = x32 * lax.rsqrt(jnp.mean(x32 * x32, axis=-1, keepdims=True) + EPS)
    return (y * g.astype(jnp.float32)).astype(x.dtype)


def hgrn2_scan(q, k, v, logf, s0):
    B, T, H, _ = q.shape
    n = T // CHUNK

    def to_chunks(a):
        return a.astype(jnp.float32).reshape(B, n, CHUNK, H, a.shape[-1]).transpose(1, 0, 3, 2, 4)

    qc, kc, vc, gc = to_chunks(q), to_chunks(k), to_chunks(v), to_chunks(logf)
    lower = jnp.tril(jnp.ones((CHUNK, CHUNK), dtype=bool))[:, :, None]

    def step(S, inp):
        qt, kt, vt, gt = inp
        b = jnp.cumsum(gt, axis=-2)
        o_inter = jnp.einsum('bhtd,bhde->bhte', qt * jnp.exp(b), S)
        diff = b[:, :, :, None, :] - b[:, :, None, :, :]
        decay = jnp.exp(jnp.where(lower, diff, -jnp.inf))
        scores = jnp.einsum('bhtd,bhsd,bhtsd->bhts', qt, kt, decay)
        o_intra = jnp.einsum('bhts,bhse->bhte', scores, vt)
        b_last = b[:, :, -1, :]
        S_new = jnp.exp(b_last)[..., None] * S + jnp.einsum(
            'bhsd,bhse->bhde', kt * jnp.exp(b_last[:, :, None, :] - b), vt)
        return S_new, o_inter + o_intra

    S_fin, o = lax.scan(step, s0.astype(jnp.float32), (qc, kc, vc, gc))
    o = o.transpose(1, 0, 3, 2, 4).reshape(B, T, H, -1)
    return o, S_fin


def hgrn2_branch(zq, zf_fwd, zf_bwd, zi, zog, lb, norm_g, s0_fwd, s0_bwd):
    B, T, _ = zq.shape
    heads = lambda a: a.reshape(B, T, N_HEADS_A, HEAD_A)

    def gate(zf, lbd):
        f = lbd + (1.0 - lbd) * jax.nn.sigmoid(zf.astype(jnp.float32))
        return heads(jnp.log(f)), heads(1.0 - f)

    q, v = heads(zq), heads(zi)
    logf_f, k_f = gate(zf_fwd, lb[0])
    logf_b, k_b = gate(zf_bwd, lb[1])
    o_f, S_f = hgrn2_scan(q, k_f, v, logf_f, s0_fwd)
    flip = lambda a: jnp.flip(a, axis=1)
    o_b, S_b = hgrn2_scan(flip(q), flip(k_b), flip(v), flip(logf_b), s0_bwd)
    o = o_f + flip(o_b)
    o = o * lax.rsqrt(jnp.mean(o * o, axis=-1, keepdims=True) + EPS)
    o = o * norm_g.astype(jnp.float32).reshape(N_HEADS_A, HEAD_A)
    o = o.reshape(B, T, D_A) * jax.nn.silu(zog.astype(jnp.float32))
    return o.astype(zq.dtype), S_f, S_b


def pool_minus_self_1d(x):
    B, T, _ = x.shape
    x32 = x.astype(jnp.float32)
    cs = jnp.concatenate([jnp.zeros((B, 1, D_B), jnp.float32), jnp.cumsum(x32, axis=1)], axis=1)
    pos = jnp.arange(T)
    outs = []
    for g, w in enumerate(POOL_WINDOWS):
        lo = jnp.clip(pos - w // 2, 0, T - 1)
        hi = jnp.clip(pos + w // 2 - 1, 0, T - 1)
        csg = cs[..., g * POOL_GROUP:(g + 1) * POOL_GROUP]
        s = jnp.take(csg, hi + 1, axis=1) - jnp.take(csg, lo, axis=1)
        cnt = (hi - lo + 1).astype(jnp.float32)[None, :, None]
        outs.append(s / cnt)
    return (jnp.concatenate(outs, axis=-1) - x32).astype(x.dtype)


def pool_minus_self_2d(x):
    B, T, _ = x.shape
    rows = T // GRID_W
    xg = x.astype(jnp.float32).reshape(B, rows, GRID_W, D_B)
    sat = jnp.pad(jnp.cumsum(jnp.cumsum(xg, axis=1), axis=2), ((0, 0), (1, 0), (1, 0), (0, 0)))
    r, cidx = jnp.arange(rows), jnp.arange(GRID_W)
    outs = []
    for g, w in enumerate(POOL_WINDOWS):
        rlo, rhi = jnp.clip(r - w // 2, 0, rows - 1), jnp.clip(r + w // 2 - 1, 0, rows - 1)
        clo, chi = jnp.clip(cidx - w // 2, 0, GRID_W - 1), jnp.clip(cidx + w // 2 - 1, 0, GRID_W - 1)
        sg = sat[..., g * POOL_GROUP:(g + 1) * POOL_GROUP]
        rect = lambda ri, ci: jnp.take(jnp.take(sg, ri, axis=1), ci, axis=2)
        s = rect(rhi + 1, chi + 1) - rect(rlo, chi + 1) - rect(rhi + 1, clo) + rect(rlo, clo)
        cnt = ((rhi - rlo + 1)[:, None] * (chi - clo + 1)[None, :]).astype(jnp.float32)
        outs.append(s / cnt[None, :, :, None])
    pooled = jnp.concatenate(outs, axis=-1)
    return (pooled - xg).reshape(B, T, D_B).astype(x.dtype)


def trunk_layer(x, mod, s0_fwd, s0_bwd, grid, lb, norm1_g, w_in, hgrn_norm_g, w_branch_a,
                pool_w, pool_scale, w_branch_b, w_out, norm2_g, w_ffn_in, w_ffn_out):
    B, T, _ = x.shape
    shift1, scale1, gate1, shift2, scale2, gate2 = [m[:, None, :] for m in jnp.split(mod, 6, axis=-1)]
    h = rmsnorm(x, norm1_g) * (1.0 + scale1) + shift1
    zq, zf_fwd, zf_bwd, zi, zog, zpool, zga, zgb = jnp.split(h @ w_in, SPLITS, axis=-1)
    o_a, S_f, S_b = hgrn2_branch(zq, zf_fwd, zf_bwd, zi, zog, lb, hgrn_norm_g, s0_fwd, s0_bwd)
    pm = pool_minus_self_2d(zpool) if grid else pool_minus_self_1d(zpool)
    o_b = jnp.einsum('btgc,gcd->btgd', pm.reshape(B, T, N_POOL_GROUPS, POOL_GROUP), pool_w)
    o_b = o_b.reshape(B, T, D_B) * pool_scale
    merged = jax.nn.sigmoid(zga) * (o_a @ w_branch_a) + jax.nn.sigmoid(zgb) * (o_b @ w_branch_b)
    x = x + gate1 * (merged @ w_out)
    h2 = rmsnorm(x, norm2_g) * (1.0 + scale2) + shift2
    g_part, u_part = jnp.split(h2 @ w_ffn_in, 2, axis=-1)
    x = x + gate2 * ((jax.nn.silu(g_part) * u_part) @ w_ffn_out)
    return x, S_f, S_b


def setup_inputs(seed: int = 0) -> dict:
    key = jax.random.key(seed)
    ks = jax.random.split(key, 20)
    nrm = lambda k, shape, scale: jax.random.normal(k, shape, jnp.float32) * scale
    return {
        'x_prompt': nrm(ks[0], (BATCH, SEQ, D_MODEL), 1.0),
        'x_sample': nrm(ks[1], (DEC_BATCH, DEC_SEQ, D_MODEL), 1.0),
        'state_hgrn': nrm(ks[2], (DEC_BATCH, DEPTH, N_DIRS, N_HEADS_A, HEAD_A, HEAD_A), 0.5),
        'c': nrm(ks[3], (DEC_BATCH, D_MODEL), 1.0),
        'c_ctx': nrm(ks[4], (D_MODEL,), 1.0),
        'w_ada': nrm(ks[5], (DEPTH, D_MODEL, 6 * D_MODEL), D_MODEL ** -0.5),
        'b_ada': nrm(ks[6], (DEPTH, 6 * D_MODEL), 0.02),
        'norm1_g': 1.0 + nrm(ks[7], (DEPTH, D_MODEL), 0.05),
        'w_in': nrm(ks[8], (DEPTH, D_MODEL, IN_COLS), D_MODEL ** -0.5),
        'hgrn_lb_logits': nrm(ks[9], (DEPTH + 1, N_DIRS, D_A), 0.5),
        'hgrn_norm_g': 1.0 + nrm(ks[10], (DEPTH, D_A), 0.05),
        'w_branch_a': nrm(ks[11], (DEPTH, D_A, D_MODEL), D_A ** -0.5),
        'pool_w': nrm(ks[12], (DEPTH, N_POOL_GROUPS, POOL_GROUP, POOL_GROUP), POOL_GROUP ** -0.5),
        'pool_scale': 1.0 + nrm(ks[13], (DEPTH, D_B), 0.05),
        'w_branch_b': nrm(ks[14], (DEPTH, D_B, D_MODEL), D_B ** -0.5),
        'w_out': nrm(ks[15], (DEPTH, D_MODEL, D_MODEL), D_MODEL ** -0.5),
        'norm2_g': 1.0 + nrm(ks[16], (DEPTH, D_MODEL), 0.05),
        'w_ffn_in': nrm(ks[17], (DEPTH, D_MODEL, 2 * D_FF), D_MODEL ** -0.5),
        'w_ffn_out': nrm(ks[18], (DEPTH, D_FF, D_MODEL), D_FF ** -0.5),
        'final_g': 1.0 + nrm(ks[19], (D_MODEL,), 0.05),
    }


def reference(x_prompt, x_sample, state_hgrn, c, c_ctx, w_ada, b_ada, norm1_g, w_in,
              hgrn_lb_logits, hgrn_norm_g, w_branch_a, pool_w, pool_scale, w_branch_b,
              w_out, norm2_g, w_ffn_in, w_ffn_out, final_g):
    lb_all = jnp.cumsum(jax.nn.softmax(hgrn_lb_logits.astype(jnp.float32), axis=0), axis=0)
    B_p = x_prompt.shape[0]
    zeros_state = jnp.zeros((B_p, N_HEADS_A, HEAD_A, HEAD_A), jnp.float32)
    xp, xs = x_prompt, x_sample
    ctx_states = []
    for l in range(DEPTH):
        layer_w = (lb_all[l], norm1_g[l], w_in[l], hgrn_norm_g[l], w_branch_a[l], pool_w[l],
                   pool_scale[l], w_branch_b[l], w_out[l], norm2_g[l], w_ffn_in[l], w_ffn_out[l])
        mod_ctx = jax.nn.silu(c_ctx[None, :]) @ w_ada[l] + b_ada[l]
        mod_lat = jax.nn.silu(c) @ w_ada[l] + b_ada[l]
        xp, S_f, S_b = trunk_layer(xp, mod_ctx, zeros_state, zeros_state, False, *layer_w)
        ctx_states.append(jnp.stack([S_f, S_b], axis=1))
        xs, _, _ = trunk_layer(xs, mod_lat, state_hgrn[:, l, 0], state_hgrn[:, l, 1], True, *layer_w)
    y_prompt = rmsnorm(xp, final_g)
    y_sample = rmsnorm(xs, final_g)
    new_state_hgrn = jnp.stack(ctx_states, axis=1).astype(x_prompt.dtype)
    return (y_prompt, y_sample, new_state_hgrn)
```

```python
import numpy as np
import ml_dtypes
import concourse.bass as bass
import concourse.mybir as mybir
from concourse.bass_utils import run_bass_kernel_spmd

F32 = mybir.dt.float32
BF16 = mybir.dt.bfloat16
AF = mybir.ActivationFunctionType
ALU = mybir.AluOpType

D = 1024
KC = 8
T = 1280
NT = 10
D_A = 512
D_FF = 2816
FC = 22
IN_COLS = 5120
EPS = 1e-6
CH = 64
NCH = 20
TGS = [(0, 512, 0), (512, 512, 0), (1024, 256, 1)]
N_CORES = 8
SLOT = 4096
NSLOT = 2
SYNC_SAME_ENGINE = True


class Buf:
    __slots__ = ("name", "w", "rs")

    def __init__(self, name):
        self.name = name
        self.w = None
        self.rs = []


class Strong:
    __slots__ = ("b",)

    def __init__(self, b):
        self.b = b


class Op:
    __slots__ = ("eng", "fn", "deps", "key", "sig", "tick", "dcount")

    def __init__(self, eng, fn, deps, key):
        self.eng = eng
        self.fn = fn
        self.deps = deps
        self.key = key
        self.sig = False
        self.tick = 0
        self.dcount = 0


class Prog:
    def __init__(self):
        self.ops = []
        self.dma_counts = {}

    def add(self, eng, fn, reads=(), writes=(), key=None):
        idx = len(self.ops)
        deps = {}
        strong = [b.b for b in writes if isinstance(b, Strong)]
        writes = [b.b if isinstance(b, Strong) else b for b in writes]
        for b in strong:
            if b.w is not None:
                deps[b.w] = "raw"
            for r in b.rs:
                deps[r] = "raw"

        def dep(i, kind):
            if i is None:
                return
            if i not in deps or kind == "raw":
                deps[i] = kind

        for b in reads:
            dep(b.w, "raw")
        for b in writes:
            dep(b.w, "waw")
            for r in b.rs:
                dep(r, "war")
        for b in reads:
            b.rs.append(idx)
        for b in writes:
            b.w = idx
            b.rs = []
        op = Op(eng, fn, deps, key)
        if key is not None:
            self.dma_counts[key] = self.dma_counts.get(key, 0) + 1
            op.dcount = self.dma_counts[key]
        self.ops.append(op)
        return idx

    @staticmethod
    def _needs(p, q, kind):
        if p.key is not None:
            return True
        if q.key is not None:
            return True
        if p.eng == q.eng and (p.eng == "pe" or (kind != "raw" and not SYNC_SAME_ENGINE)):
            return False
        return True

    def emit(self, nc, stack, final_keys):
        ops = self.ops
        for q in ops:
            best = {}
            for i, kind in q.deps.items():
                p = ops[i]
                if p.key is None and self._needs(p, q, kind):
                    if best.get(p.eng, -1) < i:
                        best[p.eng] = i
            q.deps = {i: k_ for i, k_ in q.deps.items() if ops[i].key is not None or best.get(ops[i].eng) == i}
            for i in best.values():
                ops[i].sig = True
        cnt = {}
        for p in ops:
            if p.sig:
                cnt[p.eng] = cnt.get(p.eng, 0) + 1
                p.tick = cnt[p.eng]
        eng_sem = {e: stack.enter_context(nc.semaphore("tick_" + e)) for e in ("pe", "act", "dve", "pool")}
        dma_sem = {k: stack.enter_context(nc.semaphore("dma_%s" % (k,))) for k in self.dma_counts}
        block = stack.enter_context(nc.Block())

        def run_engine(name, h):
            seen = {}
            for q in ops:
                if q.eng != name:
                    continue
                waits = {}
                for i, kind in q.deps.items():
                    p = ops[i]
                    if p.key is not None:
                        s, v = dma_sem[p.key], 16 * p.dcount
                    else:
                        if not self._needs(p, q, kind):
                            continue
                        s, v = eng_sem[p.eng], p.tick
                    if waits.get(s, (0,))[0] < v:
                        waits[s] = (v,)
                for s, (v,) in waits.items():
                    if seen.get(s, 0) >= v:
                        continue
                    h.wait_ge(s, v)
                    seen[s] = v
                ins = q.fn(h)
                if q.key is not None:
                    ins.then_inc(dma_sem[q.key], 16)
                elif q.sig:
                    ins.then_inc(eng_sem[q.eng], 1)
            if name == "sp":
                for k in final_keys:
                    h.wait_ge(dma_sem[k], 16 * self.dma_counts[k])

        @block.tensor
        def _(h):
            run_engine("pe", h)

        @block.scalar
        def _(h):
            run_engine("act", h)

        @block.vector
        def _(h):
            run_engine("dve", h)

        @block.gpsimd
        def _(h):
            run_engine("pool", h)

        @block.sync
        def _(h):
            run_engine("sp", h)


def _pool_consts():
    wins = (2, 4, 8, 16)
    p1 = np.zeros((4, 256, 256), np.float32)
    i1 = np.zeros((4, 256), np.float32)
    t = np.arange(256)
    for g, w in enumerate(wins):
        lo = np.clip(t - w // 2, 0, 255)
        hi = np.clip(t + w // 2 - 1, 0, 255)
        for tt in range(256):
            p1[g, tt, lo[tt]:hi[tt] + 1] = 1.0
            cnt = hi[tt] - lo[tt] + 1
            p1[g, tt, tt] -= cnt
            i1[g, tt] = 1.0 / cnt
    p2 = np.zeros((4, 1024, 1024), np.float32)
    i2 = np.zeros((4, 1024), np.float32)
    r = np.arange(16)
    c = np.arange(64)
    for g, w in enumerate(wins):
        rlo, rhi = np.clip(r - w // 2, 0, 15), np.clip(r + w // 2 - 1, 0, 15)
        clo, chi = np.clip(c - w // 2, 0, 63), np.clip(c + w // 2 - 1, 0, 63)
        blk = p2[g].reshape(16, 64, 16, 64)
        for rr in range(16):
            for cc in range(64):
                blk[rr, cc, rlo[rr]:rhi[rr] + 1, clo[cc]:chi[cc] + 1] = 1.0
                cnt = (rhi[rr] - rlo[rr] + 1) * (chi[cc] - clo[cc] + 1)
                blk[rr, cc, rr, cc] -= cnt
                i2[g, rr * 64 + cc] = 1.0 / cnt
    return p1, i1, p2, i2


def _scan_consts():
    s = np.arange(128)[:, None]
    t = np.arange(128)[None, :]
    same = (s // CH) == (t // CH)
    mf = (same & (s <= t)).astype(np.float32)
    mb = (same & (s >= t)).astype(np.float32)
    mf = np.tile(mf, (1, 4))
    mb = np.tile(mb, (1, 4))
    rmask = np.ones((128, 512), np.float32)
    rmask[:, ::CH] = 0.0
    return mf, mb, rmask


def _colT(v, n):
    return np.ascontiguousarray(np.asarray(v, np.float32).reshape(n, 128).T)


def build_nc(debug=False):
    from contextlib import ExitStack

    nc = bass.Bass("TRN2", target_bir_lowering=False)
    P = Prog()

    def din(name, shape, dt=F32):
        return nc.dram_tensor(name, list(shape), dt, kind="ExternalInput").ap()

    x_d = din("x", [T, D])
    cT_d = din("cT", [128, 16])
    s0_d = din("s0", [5, 2, 4, 128, 128])
    flags_d = din("flags", [128, 10])
    poolA_d = din("poolA", [4, 1024, 1024], BF16)
    poolB_d = din("poolB", [4, 256, 256], BF16)
    invc_d = din("invc", [4, T])
    mf_d = din("mf", [128, 512], BF16)
    mb_d = din("mb", [128, 512], BF16)
    rmask_d = din("rmask", [128, 512])
    id32_d = din("id32", [128, 128])
    w_ada_d = din("w_ada", [D, 6 * D])
    b_adaT_d = din("b_adaT", [128, 48])
    n1gT_d = din("n1gT", [128, 8])
    n2gT_d = din("n2gT", [128, 8])
    fgT_d = din("fgT", [128, 8])
    w_in_d = din("w_in", [D, IN_COLS])
    lb0T_d = din("lb0T", [128, 8])
    lb1T_d = din("lb1T", [128, 8])
    hngT_d = din("hngT", [128, 4])
    wba_d = din("wba", [D_A, D])
    poolw_d = din("poolw", [4, 128, 128])
    pscT_d = din("pscT", [128, 4])
    wbb_d = din("wbb", [D_A, D])
    wout_d = din("wout", [D, D])
    n_ffin_d = din("wffin", [D, 2 * D_FF])
    wffout_d = din("wffout", [D_FF, D])
    y_d = nc.dram_tensor("y", [T, D], F32, kind="ExternalOutput").ap()
    st_d = nc.dram_tensor("st", [5, 2, 4, 128, 128], F32, kind="ExternalOutput").ap()
    if debug:
        dbg_hT = nc.dram_tensor("dbg_hT", [128, KC, T], BF16, kind="ExternalOutput").ap()
        dbg_oaT = nc.dram_tensor("dbg_oaT", [128, 4, T], BF16, kind="ExternalOutput").ap()
        dbg_obT = nc.dram_tensor("dbg_obT", [128, 4, T], BF16, kind="ExternalOutput").ap()
        dbg_qk = nc.dram_tensor("dbg_qk", [128, 4, 4, T], BF16, kind="ExternalOutput").ap()
        dbg_v = nc.dram_tensor("dbg_v", [128, NT, 512], BF16, kind="ExternalOutput").ap()

    stack = ExitStack()
    with stack:
        ARENA = 212800
        arena = nc.alloc_sbuf_tensor("arena", [128, ARENA // 4], F32)
        base = nc.lookup_mloc(arena).addr
        off = [0]

        def alloc(name, shape, dt, at=None):
            nbytes = int(np.prod(shape[1:])) * (4 if dt == F32 else 2)
            nbytes = (nbytes + 31) // 32 * 32
            if at is None:
                at = off[0]
                off[0] += nbytes
                assert off[0] <= ARENA, (name, off[0])
            return nc.alloc_sbuf_tensor_at(name, list(shape), dt, offset=base + at), at, nbytes

        xT, _, _ = alloc("xT", [128, KC, T], F32)
        hT, _, _ = alloc("hT", [128, KC, T], BF16)
        ring, _, _ = alloc("ring", [128, NSLOT, SLOT], BF16)
        ident32, _, _ = alloc("ident32", [128, 128], F32)
        identb, _, _ = alloc("identb", [128, 128], BF16)
        onesb, _, _ = alloc("onesb", [128, 128], BF16)
        mfm, _, _ = alloc("mfm", [128, 512], BF16)
        mbm, _, _ = alloc("mbm", [128, 512], BF16)
        rmask, _, _ = alloc("rmask", [128, 512], F32)
        cT, _, _ = alloc("cT", [128, 16], F32)
        scT, _, _ = alloc("scT", [128, 2, 8], BF16)
        badaT, _, _ = alloc("badaT", [128, 48], F32)
        modT, _, _ = alloc("modT", [128, 48, 2], F32)
        g1c, _, _ = alloc("g1c", [128, 8], F32)
        g2c, _, _ = alloc("g2c", [128, 8], F32)
        fgc, _, _ = alloc("fgc", [128, 8], F32)
        G1, _, _ = alloc("G1", [128, 8, 2], F32)
        G2, _, _ = alloc("G2", [128, 8, 2], F32)
        lb0, _, _ = alloc("lb0", [128, 8], F32)
        lb1, _, _ = alloc("lb1", [128, 8], F32)
        lbc, _, _ = alloc("lbc", [128, 8], F32)
        omlb, _, _ = alloc("omlb", [128, 8], F32)
        nomlb, _, _ = alloc("nomlb", [128, 8], F32)
        hng, _, _ = alloc("hng", [128, 4], F32)
        psc, _, _ = alloc("psc", [128, 4], F32)
        flg, _, _ = alloc("flg", [128, 10], F32)
        pw32, _, _ = alloc("pw32", [128, 4, 128], F32)
        pwb, _, _ = alloc("pwb", [128, 4, 128], BF16)
        sqb, _, _ = alloc("sqb", [128, 4, 512], BF16)
        rstd2, _, _ = alloc("rstd", [128, 2, 512], F32)
        ntmp, _, _ = alloc("ntmp", [128, 2, 512], F32)
        oaT, _, _ = alloc("oaT", [128, 4, T], BF16)
        obT, obo, _ = alloc("obT", [128, 4, T], BF16)
        ktokB = nc.alloc_sbuf_tensor_at("ktokB", [128, 2, 2, NT, 128], BF16, offset=base + obo)
        R0 = off[0]
        qk, _, _ = alloc("qk", [128, 2, 4, T], BF16)
        vtok, _, _ = alloc("vtok", [128, NT, 512], BF16)
        zogs, _, _ = alloc("zogs", [128, 2, T], BF16)
        ktok, _, _ = alloc("ktok", [128, 2, 2, NT, 128], BF16)
        sbf, _, _ = alloc("sbf", [128, 2, NCH, 128], BF16)
        assert off[0] - R0 == 56320, off[0] - R0
        csc, _, _ = alloc("csc", [128, 2, 2, NCH], F32)
        ctmp, _, _ = alloc("ctmp", [128, 8], F32)
        Xb, _, _ = alloc("Xb", [128, 2, 2, 128], F32)
        carry, _, _ = alloc("carry", [128, 2, 128], F32)
        s0t, _, _ = alloc("s0t", [128, 2, 2, 128], F32)
        sct, _, _ = alloc("sct", [128, 2, 512], BF16)
        off[0] = max(off[0], R0 + FC * T * 2)
        q32, q32o, _ = alloc("q32", [128, 2, 512], F32)
        minir = nc.alloc_sbuf_tensor_at("minir", [128, 2, KC * 128], BF16, offset=base + q32o)
        LFt, lfo, _ = alloc("LFt", [128, 2, 2, 520], F32)
        KKt, _, _ = alloc("KKt", [128, 2, 2, 512], BF16)
        SGt, g4o, _ = alloc("SGt", [128, 2, 2, 512], F32)
        BBt, _, _ = alloc("BBt", [128, 2, 512], F32)
        ytok = nc.alloc_sbuf_tensor_at("ytok", [128, 2, D], F32, offset=base + g4o)
        ytokB = nc.alloc_sbuf_tensor_at("ytokB", [128, 2, D], F32, offset=base + lfo)
        assert off[0] <= ARENA, off[0]
        PSL = 8 * 1024 + 2 * 256
        pstr = nc.alloc_sbuf_tensor_at("pstr", [128, 2, PSL], BF16, offset=base + R0)
        zpool = nc.alloc_sbuf_tensor_at("zpool", [128, NT, 512], BF16, offset=base + R0 + 34816)
        invb = nc.alloc_sbuf_tensor_at("invb", [128, 3, 512], F32, offset=base + R0 + 45056)
        pmT = nc.alloc_sbuf_tensor_at("pmT", [128, 3, 512], BF16, offset=base + R0 + 51200)
        assert 51200 + 3072 <= 56320
        mergedT = nc.alloc_sbuf_tensor_at("mergedT", [128, KC, T], BF16, offset=base + R0)
        sg32 = nc.alloc_sbuf_tensor_at("sg32", [128, 2, 512], F32, offset=base + R0 + 20480)
        m1 = nc.alloc_sbuf_tensor_at("m1", [128, 2, 512], F32, offset=base + R0 + 24576)
        actT = nc.alloc_sbuf_tensor_at("actT", [128, FC, T], BF16, offset=base + R0)

        banks = [nc.alloc_psum_tensor("ps%d" % i, [128, 512], F32) for i in range(8)]
        bbuf = [Buf("ps%d" % i) for i in range(8)]
        bctr = {}
        pools = {"all": list(range(8))}
        pmode = ["all"]

        def ps(pool=None):
            name = pool if (pool is not None and pmode[0] == "split") else "all"
            lst = pools[name]
            c = bctr.get(name, 0)
            bctr[name] = c + 1
            i = lst[c % len(lst)]
            return banks[i], bbuf[i]

        bufs = {}
        REG = []

        def B(*k):
            if k not in bufs:
                bufs[k] = Buf(str(k))
            return bufs[k]

        def RB(*k):
            if k not in bufs:
                bufs[k] = Buf(str(k))
                REG.append(bufs[k])
            return bufs[k]

        fence = {"act": [], "dve": [], "sp": []}

        def new_phase():
            for e in fence:
                fence[e] = list(REG)

        def fz(eng):
            r = [Strong(b_) for b_ in fence[eng]]
            fence[eng] = []
            return r

        def dma(eng, out, in_, reads, writes, key):
            P.add(eng, lambda h, o=out, i=in_: h.dma_start(out=o, in_=i), reads, writes, key)

        def act(out, in_, func, reads, writes, scale=1.0, bias=0.0):
            P.add("act", lambda h, o=out, i=in_, f=func, s=scale, b=bias: h.activation(o, i, f, bias=b, scale=s),
                  reads, writes)

        def tt(eng, out, a, b, op, reads, writes):
            P.add(eng, lambda h, o=out, a=a, b=b, op=op: h.tensor_tensor(o, a, b, op), reads, writes)

        def ts(eng, out, a, s1, s2, op0, op1, reads, writes):
            P.add(eng, lambda h, o=out, a=a, s1=s1, s2=s2, op0=op0, op1=op1: h.tensor_scalar(o, a, s1, s2, op0, op1),
                  reads, writes)

        def stt(out, a, s, b, op0, op1, reads, writes):
            P.add("dve", lambda h, o=out, a=a, s=s, b=b, op0=op0, op1=op1: h.scalar_tensor_tensor(o, a, s, b, op0, op1),
                  reads, writes)

        def cp(eng, out, in_, reads, writes):
            if eng == "act":
                P.add("act", lambda h, o=out, i=in_: h.copy(o, i), reads, writes)
            else:
                P.add(eng, lambda h, o=out, i=in_: h.tensor_copy(o, i), reads, writes)

        def mm(out, lhsT, rhs, start, stop, reads, writes):
            P.add("pe", lambda h, o=out, l=lhsT, r=rhs, s=start, e=stop: h.matmul(o, l, r, start=s, stop=e),
                  reads, writes)

        def tr(out, in_, ident, reads, writes):
            P.add("pe", lambda h, o=out, i=in_, d=ident: h.transpose(o, i, d), reads, writes)

        wblocks = []

        def wview(w_d, c0, n):
            return w_d.rearrange("(kc p) n -> p kc n", p=128)[:, :, c0:c0 + n]

        wstate = {"issued": 0}

        def slot_bufs(s):
            return [B("ring", s, pi) for pi in range(4)]

        def w_issue_upto(k):
            while wstate["issued"] <= min(k, len(wblocks) - 1):
                j = wstate["issued"]
                s = j % NSLOT
                pieces = wblocks[j]
                for pi, (o, shp, src) in enumerate(pieces):
                    n_el = int(np.prod(shp))
                    dst = ring[:, s, o:o + n_el].rearrange("p (a b) -> p a b", a=shp[0])
                    wr = slot_bufs(s) if len(pieces) == 1 else [B("ring", s, pi)]
                    dma("pool", dst, src, [], wr, ("ring", s, pi))
                wstate["issued"] += 1

        def wget(k):
            w_issue_upto(k + NSLOT - 1)
            return k % NSLOT

        def wbufs(k):
            return slot_bufs(k % NSLOT)

        WB_A = {}

        def ada_blk(b):
            WB_A[b] = len(wblocks)
            wblocks.append([(0, (KC, 512), wview(w_ada_d, b * 512, 512))])

        for b in range(4):
            ada_blk(b)
        WB_V = len(wblocks)
        wblocks.append([(0, (KC, 512), wview(w_in_d, 1536, 512))])
        WB_HL = []
        for h_ in range(4):
            WB_HL.append(len(wblocks))
            wblocks.append([(j * KC * 128, (KC, 128), wview(w_in_d, c_ + h_ * 128, 128))
                            for j, c_ in enumerate((0, 512, 1024, 2048))])
        WB_POOL = len(wblocks)
        wblocks.append([(0, (KC, 512), wview(w_in_d, 2560, 512))])
        WB_ML = []
        for n_ in range(8):
            WB_ML.append(len(wblocks))
            wblocks.append([
                (0, (KC, 128), wview(w_in_d, 3072 + n_ * 128, 128)),
                (KC * 128, (KC, 128), wview(w_in_d, 4096 + n_ * 128, 128)),
                (2 * KC * 128, (4, 128), wview(wba_d, n_ * 128, 128)),
                (2 * KC * 128 + 512, (4, 128), wview(wbb_d, n_ * 128, 128)),
            ])
        WB_O = len(wblocks)
        for b in range(2):
            wblocks.append([(0, (KC, 512), wview(wout_d, b * 512, 512))])
        WB_FL = []
        for b in range(11):
            WB_FL.append(len(wblocks))
            wblocks.append([
                (0, (KC, 128), wview(n_ffin_d, (2 * b) * 128, 128)),
                (KC * 128, (KC, 128), wview(n_ffin_d, D_FF + (2 * b) * 128, 128)),
                (2 * KC * 128, (KC, 128), wview(n_ffin_d, (2 * b + 1) * 128, 128)),
                (3 * KC * 128, (KC, 128), wview(n_ffin_d, D_FF + (2 * b + 1) * 128, 128)),
            ])
        WB_FO = len(wblocks)
        for b in range(8):
            wblocks.append([(0, (FC, 128), wview(wffout_d, b * 128, 128))])

        def rslice(s, o, shp):
            n_el = int(np.prod(shp))
            return ring[:, s, o:o + n_el].rearrange("p (a b) -> p a b", a=shp[0])

        def tg_of_tile(i):
            return 0 if i < 4 else (1 if i < 8 else 2)

        YB = [[B("SG", 0, 0), B("SG", 0, 1)], [B("SG", 1, 0), B("SG", 1, 1)]]
        small = [(ident32, id32_d), (cT, cT_d), (badaT, b_adaT_d), (g1c, n1gT_d), (mfm, mf_d), (mbm, mb_d),
                 (rmask, rmask_d), (g2c, n2gT_d), (fgc, fgT_d), (lb0, lb0T_d), (lb1, lb1T_d), (hng, hngT_d),
                 (psc, pscT_d), (flg, flags_d)]
        SM = [B("small", tns.name) for tns, _ in small] + [B("small", "pw32")]

        def x_load(i):
            sl = i % 2
            dma("sp", ytok[:, sl, :], x_d[i * 128:(i + 1) * 128, :], [], YB[sl], ("ytok", sl))

        dma("sp", ident32[:], id32_d, [], [B("small", "ident32")], ("small",))
        x_load(0)
        x_load(1)
        for tns, src_ in small[1:]:
            dma("sp", tns[:], src_, [], [B("small", tns.name)], ("small",))
        dma("sp", pw32[:], poolw_d.rearrange("g c d -> c g d"), [], [B("small", "pw32")], ("small",))

        P.add("pool", lambda h: h.memset(ktok[:].rearrange("p a b c d -> p (a b c d)"), 0.0), [], [RB("ktokz", 0)])
        P.add("pool", lambda h: h.memset(ktokB[:].rearrange("p a b c d -> p (a b c d)"), 0.0), [],
              [RB("ktokz", 1)])
        P.add("dve", lambda h: h.tensor_copy(identb[:], ident32[:]), SM, [B("identb")])
        P.add("dve", lambda h: h.memset(onesb[:], 1.0), [], [B("onesb")])
        P.add("dve", lambda h: h.tensor_copy(pwb[:], pw32[:]), SM, [B("pwb")])
        P.add("dve", lambda h: h.memset(LFt[:, :, :, 0:8], 0.0), [],
              [B("LF", a, b) for a in range(2) for b in range(2)])
        tt("dve", lb1[:], lb0[:], lb1[:], ALU.subtract, SM, [B("lbd")])
        act(lbc[:], lb1[:], AF.Sigmoid, [B("lbd")], [B("lbc")])
        ts("dve", omlb[:], lbc[:], -1.0, 1.0, ALU.mult, ALU.add, [B("lbc")], [B("omlb")])
        ts("dve", nomlb[:], lbc[:], 1.0, -1.0, ALU.mult, ALU.add, [B("lbc")], [B("omlb")])
        act(scT[:].rearrange("p g k -> p (g k)"), cT[:], AF.Silu, SM, [B("scT")])

        def x_tile(i):
            sl = i % 2
            for hf in range(2):
                pt, pb = ps()
                for j in range(4):
                    kc = hf * 4 + j
                    tr(pt[:, j * 128:(j + 1) * 128], ytok[:, sl, kc * 128:(kc + 1) * 128], ident32[:],
                       YB[sl] + SM, [pb])
                cp("dve" if hf == 0 else "act", xT[:, hf * 4:hf * 4 + 4, i * 128:(i + 1) * 128],
                   pt[:, :].rearrange("p (a b) -> p a b", a=4), [pb],
                   [B("xT", hf * 4 + j, tg_of_tile(i)) for j in range(4)])
            if i + 2 < NT:
                x_load(i + 2)

        def MB(c):
            return B("modblk", c // 4)

        def adaln_block(bb):
            k = WB_A[bb]
            s = wget(k)
            wv = rslice(s, 0, (KC, 512))
            pt, pb = ps()
            for j in range(4):
                for kc in range(KC):
                    mm(pt[:, 2 * j:2 * j + 2], wv[:, kc, j * 128:(j + 1) * 128], scT[:, :, kc],
                       kc == 0, kc == KC - 1, wbufs(k) + [B("scT")], [pb])
            tt("dve", modT[:, bb * 4:bb * 4 + 4, :], pt[:, 0:8].rearrange("p (c g) -> p c g", g=2),
               badaT[:, bb * 4:bb * 4 + 4].unsqueeze(2).broadcast_to([128, 4, 2]), ALU.add,
               [pb] + SM, [B("modblk", bb)])

        mini = {"next": 16}

        def mini_issue_upto(c):
            while mini["next"] <= min(c, 47):
                cc = mini["next"]
                sl_ = cc % 2
                dst = minir[:, sl_, :].rearrange("p (a b) -> p a b", a=KC)
                dma("pool", dst, wview(w_ada_d, cc * 128, 128), [], [B("q32", sl_)], ("mini", sl_))
                mini["next"] += 1

        def mini_consume(c):
            mini_issue_upto(c + 1)
            sl_ = c % 2
            pt, pb = ps()
            for kc in range(KC):
                mm(pt[:, 0:2], minir[:, sl_, kc * 128:(kc + 1) * 128], scT[:, :, kc], kc == 0, kc == KC - 1,
                   [B("q32", sl_), B("scT")], [pb])
            ts("dve", modT[:, c, :], pt[:, 0:2], badaT[:, c:c + 1], None, ALU.add, ALU.bypass, [pb] + SM, [MB(c)])

        def make_G(Gt, gcol, sc0, name):
            for g in range(2):
                stt(Gt[:, :, g], modT[:, sc0:sc0 + 8, g], 1.0, gcol[:], ALU.add, ALU.mult,
                    [MB(sc0), MB(sc0 + 4)] + SM, [B("G", name)])

        sq_eng = ["act"]

        def rms_rstd(tg, slot=None):
            t0, n, g = TGS[tg]
            pt, pb = ps()
            for hf in range(2):
                if sq_eng[0] == "act":
                    act(sqb[:, :, 0:n], xT[:, hf * 4:hf * 4 + 4, t0:t0 + n], AF.Square,
                        [B("xT", hf * 4 + j, tg) for j in range(4)], [B("sqb")])
                else:
                    tt("pool", sqb[:, :, 0:n], xT[:, hf * 4:hf * 4 + 4, t0:t0 + n],
                       xT[:, hf * 4:hf * 4 + 4, t0:t0 + n], ALU.mult,
                       [B("xT", hf * 4 + j, tg) for j in range(4)], [B("sqb")])
                for j in range(4):
                    mm(pt[:, 0:n], onesb[:], sqb[:, j, 0:n], hf == 0 and j == 0, hf == 1 and j == 3,
                       [B("sqb"), B("onesb")], [pb])
            rs = tg % 2 if slot is None else slot
            act(rstd2[:, rs, 0:n], pt[:, 0:n], AF.Ln, [pb], [B("rstd", rs)], scale=1.0 / D, bias=EPS)
            act(rstd2[:, rs, 0:n], rstd2[:, rs, 0:n], AF.Exp, [B("rstd", rs)], [B("rstd", rs)], scale=-0.5)
            return rs

        def modulate(Gt, name, shift_c0, tg, rs):
            t0, n, g = TGS[tg]
            for kc in range(KC):
                sl = kc % 2
                stt(ntmp[:, sl, 0:n], xT[:, kc, t0:t0 + n], Gt[:, kc, g:g + 1], rstd2[:, rs, 0:n], ALU.mult, ALU.mult,
                    [B("xT", kc, tg), B("rstd", rs), B("G", name)], [B("ntmp", sl)])
                act(hT[:, kc, t0:t0 + n], ntmp[:, sl, 0:n], AF.Identity,
                    [B("ntmp", sl), MB(shift_c0), MB(shift_c0 + 4)], [B("hT", kc, tg)],
                    bias=modT[:, shift_c0 + kc, g:g + 1])

        def norm_tg(Gt, name, shift_c0, tg):
            rs = rms_rstd(tg)
            modulate(Gt, name, shift_c0, tg, rs)

        def norm_to_hT(Gt, name, shift_c0):
            for tg in range(3):
                norm_tg(Gt, name, shift_c0, tg)

        x_tile(0)
        x_tile(1)
        adaln_block(0)
        x_tile(2)
        x_tile(3)
        r0 = rms_rstd(0, 0)
        adaln_block(1)
        x_tile(4)
        x_tile(5)
        adaln_block(2)
        x_tile(6)
        x_tile(7)
        r1 = rms_rstd(1, 1)
        adaln_block(3)
        make_G(G1, g1c, 8, "1")
        modulate(G1, "1", 0, 0, r0)
        x_tile(8)
        x_tile(9)
        r2 = rms_rstd(2, 0)
        modulate(G1, "1", 0, 1, r1)
        modulate(G1, "1", 0, 2, r2)
        sq_eng[0] = "pool"
        if debug:
            dma("sp", dbg_hT, hT[:], [B("hT", kc, tg) for kc in range(KC) for tg in range(3)], [], ("dbg",))

        def hT_bufs(tg):
            return [B("hT", kc, tg) for kc in range(KC)]

        def tok_major(k, dst, dname, tiles=None):
            s = wget(k)
            wv = rslice(s, 0, (KC, 512))
            for i in (range(NT) if tiles is None else tiles):
                pt, pb = ps()
                for kc in range(KC):
                    mm(pt[:, :], hT[:, kc, i * 128:(i + 1) * 128], wv[:, kc, :], kc == 0, kc == KC - 1,
                       wbufs(k) + hT_bufs(tg_of_tile(i)), [pb])
                eng = "dve" if i % 2 == 0 else "act"
                cp(eng, dst[:, i, :], pt[:, :], [pb], [RB(dname, i)] + fz(eng))

        def feat_major(wv_fn, rbufs, tg, pool=None):
            t0, n, g = TGS[tg]
            pt, pb = ps(pool)
            for kc in range(KC):
                mm(pt[:, 0:n], wv_fn(kc), hT[:, kc, t0:t0 + n], kc == 0, kc == KC - 1, rbufs + hT_bufs(tg), [pb])
            return pt, pb

        new_phase()
        tok_major(WB_V, vtok, "vtok")

        qctr = [0]

        def gates(hh):
            k = WB_HL[hh]
            s = wget(k)
            qsl = hh % 2
            CS = RB("csc", qsl)
            wo = rslice(s, 3 * KC * 128, (KC, 128))
            zp = [feat_major(lambda kc, w=wo: w[:, kc, :], wbufs(k), tg, "g") for tg in range(3)]
            for tg in range(3):
                t0, n, g = TGS[tg]
                act(zogs[:, qsl, t0:t0 + n], zp[tg][0][:, 0:n], AF.Silu, [zp[tg][1]],
                    [RB("zogs", qsl, tg)] + fz("act"))
            yield
            wq = rslice(s, 0, (KC, 128))
            wfs = [rslice(s, (1 + dr) * KC * 128, (KC, 128)) for dr in range(2)]

            def proj(tg, which):
                w = wq if which == 0 else wfs[which - 1]
                return feat_major(lambda kc, w=w: w[:, kc, :], wbufs(k), tg, "g")

            nxt = [proj(0, 0), proj(0, 1), proj(0, 2)]
            yield
            for tg in range(3):
                t0, n, g = TGS[tg]
                nch = n // CH
                c0 = t0 // CH
                qs = qctr[0] % 2
                qctr[0] += 1
                curp = nxt
                nxt = []
                cp("act", q32[:, qs, 0:n], curp[0][0][:, 0:n], [curp[0][1]], [B("q32", qs)])
                pf = [curp[1], curp[2]]
                pr = qs
                SGv = [SGt[:, pr, dr, 0:n] for dr in range(2)]
                LFv = [LFt[:, pr, dr, 8:8 + n] for dr in range(2)]
                LFs = [LFt[:, pr, dr, 7:7 + n] for dr in range(2)]
                BBv = [BBt[:, dr, 0:n] for dr in range(2)]
                KKv = [KKt[:, pr, dr, 0:n] for dr in range(2)]
                bSG = [B("SG", pr, dr) for dr in range(2)]
                bLF = [B("LF", pr, dr) for dr in range(2)]
                bKK = [B("KK", pr, dr) for dr in range(2)]
                li = [dr * 4 + hh for dr in range(2)]
                for dr in range(2):
                    act(SGv[dr], pf[dr][0][:, 0:n], AF.Sigmoid, [pf[dr][1]], [bSG[dr]])
                yield
                if tg < 2:
                    nxt.append(proj(tg + 1, 0))
                for dr in range(2):
                    act(LFv[dr], SGv[dr], AF.Ln, [bSG[dr], B("omlb"), B("lbc")], [bLF[dr]],
                        scale=omlb[:, li[dr]:li[dr] + 1], bias=lbc[:, li[dr]:li[dr] + 1])
                    ts("dve", KKv[dr], SGv[dr], nomlb[:, li[dr]:li[dr] + 1], omlb[:, li[dr]:li[dr] + 1], ALU.mult,
                       ALU.add, [bSG[dr], B("omlb")], [bKK[dr]])
                yield
                if tg < 2:
                    nxt.append(proj(tg + 1, 1))
                P.add("dve", lambda h, n=n, o=BBv[0], l=LFv[0]: h.tensor_tensor_scan(o, rmask[:, 0:n], l, 0.0,
                                                                                     ALU.mult, ALU.add),
                      [bLF[0]] + SM, [B("BB", 0)])
                P.add("dve", lambda h, n=n, o=BBv[1], l=LFs[1]: h.tensor_tensor_scan(o, l, rmask[:, 0:n], 0.0,
                                                                                     ALU.add, ALU.mult),
                      [bLF[1]] + SM, [B("BB", 1)])
                yield
                if tg < 2:
                    nxt.append(proj(tg + 1, 2))
                b3 = [BBv[dr].rearrange("p (c t) -> p c t", t=CH) for dr in range(2)]
                l3 = LFv[1].rearrange("p (c t) -> p c t", t=CH)
                act(csc[:, qsl, 0, c0:c0 + nch], b3[0][:, :, 63], AF.Exp, [B("BB", 0)], [CS])
                tt("dve", ctmp[:, 0:nch], b3[1][:, :, 63], l3[:, :, 63], ALU.add, [B("BB", 1), bLF[1]],
                   [RB("ctmp")])
                act(csc[:, qsl, 1, c0:c0 + nch], ctmp[:, 0:nch], AF.Exp, [RB("ctmp")], [CS])
                for dr in range(2):
                    sg = 1.0 if dr == 0 else -1.0
                    act(SGv[dr], BBv[dr], AF.Exp, [B("BB", dr)], [bSG[dr]], scale=sg)
                yield
                for dr in range(2):
                    sg = -1.0 if dr == 0 else 1.0
                    act(LFv[dr], BBv[dr], AF.Exp, [B("BB", dr)], [bLF[dr]], scale=sg)
                    tt("dve", qk[:, qsl, 2 * dr, t0:t0 + n], q32[:, qs, 0:n], SGv[dr], ALU.mult,
                       [B("q32", qs), bSG[dr]], [RB("qk", qsl, 2 * dr, tg)] + fz("dve"))
                yield
                for dr in range(2):
                    tt("dve", qk[:, qsl, 2 * dr + 1, t0:t0 + n], KKv[dr], LFv[dr], ALU.mult,
                       [bKK[dr], bLF[dr]], [RB("qk", qsl, 2 * dr + 1, tg)])
                yield

        def transposes(hh):
            qsl = hh % 2
            kt = ktok if qsl == 0 else ktokB
            if debug:
                dma("sp", dbg_qk[:, hh], qk[:, qsl], [RB("qk", qsl, a, b) for a in range(4) for b in range(3)], [],
                    ("dbg",))
                if hh == 0:
                    dma("sp", dbg_v, vtok[:], [RB("vtok", i) for i in range(NT)], [], ("dbg",))
            for dr in range(2):
                for i0 in (0, 8):
                    nt_ = min(8, NT - i0)
                    pt, pb = ps("o")
                    ptb = pt.bitcast(BF16)
                    for i in range(i0, i0 + nt_):
                        tr(ptb[:, (i - i0) * 128:(i - i0 + 1) * 128], qk[:, qsl, 2 * dr + 1, i * 128:(i + 1) * 128],
                           identb[:], [RB("qk", qsl, 2 * dr + 1, tg_of_tile(i)), B("identb")], [pb])
                    eng = "act" if dr == 0 else "dve"
                    for hf in range(2):
                        rs = slice(hf * 64, (hf + 1) * 64)
                        cp(eng, kt[rs, hf, dr, i0:i0 + nt_, :],
                           ptb[rs, 0:nt_ * 128].rearrange("p (a b) -> p a b", a=nt_), [pb, RB("ktokz", qsl)],
                           [RB("ktok", qsl, dr, i0)])
                    yield

        S_init_done = [False, False]
        cur = [0, 0]
        s0cnt = [0, 0]
        s0slot = {}

        def s0_issue(hh, dr, seg):
            if hh > 3 or (hh, dr, seg) in s0slot:
                return
            ssl = s0cnt[dr] % 2
            s0cnt[dr] += 1
            s0slot[(hh, dr, seg)] = ssl
            dma("sp", s0t[:, dr, ssl, :], s0_d[seg, dr, hh], [], [RB("s0t", dr, ssl)], ("s0t", dr, ssl))

        def s0_next(hh, dr, seg):
            if dr == 0:
                return (hh, 0, seg + 1) if seg < 4 else (hh + 1, 0, 0)
            return (hh, 1, seg - 1) if seg > 0 else (hh + 1, 1, 4)

        def state_pass(hh):
            qsl = hh % 2
            kt = ktok if qsl == 0 else ktokB
            CS = RB("csc", qsl)
            for step in range(NCH):
                for dr in range(2):
                    j = step if dr == 0 else NCH - 1 - step
                    seg = j // 4
                    first = (j % 4 == 0) if dr == 0 else (j % 4 == 3)
                    last = (j % 4 == 3) if dr == 0 else (j % 4 == 0)
                    i, hf = j // 2, j % 2
                    xb = RB("X", dr)
                    a_, b_ = cur[dr], 1 - cur[dr]
                    if first:
                        s0_issue(hh, dr, seg)
                        ssl = s0slot[(hh, dr, seg)]
                        if not S_init_done[dr]:
                            P.add("dve", lambda h, dr=dr: h.memset(carry[:, dr, :], 0.0), [], [RB("carry", dr)])
                            S_init_done[dr] = True
                        fcol = (seg if dr == 0 else 5 + seg)
                        stt(Xb[:, dr, a_, :], carry[:, dr, :], flg[:, fcol:fcol + 1], s0t[:, dr, ssl, :],
                            ALU.mult, ALU.add, [RB("carry", dr), RB("s0t", dr, ssl)] + SM, [xb])
                        s0_issue(*s0_next(hh, dr, seg))
                    pt, pb = ps("st")
                    mm(pt[:, 0:128], kt[:, hf, dr, i, :], vtok[:, i, hh * 128:(hh + 1) * 128], True, True,
                       [RB("ktok", qsl, dr, 0 if i < 8 else 8), RB("vtok", i), RB("ktokz", qsl)], [pb])
                    if dr == 0 and first:
                        cj = 1.0
                    elif dr == 0:
                        cj = csc[:, qsl, 0, j - 1:j]
                    else:
                        cj = csc[:, qsl, 1, j:j + 1]
                    ts("pool", sbf[:, dr, j, :], Xb[:, dr, a_, :], cj, 0.0, ALU.mult, ALU.add, [xb, CS],
                       [RB("sbf", dr, j // 8)])
                    stt(Xb[:, dr, b_, :], Xb[:, dr, a_, :], cj, pt[:, 0:128], ALU.mult, ALU.add, [xb, pb, CS], [xb])
                    cur[dr] = b_
                    if last:
                        if dr == 0:
                            ts("dve", carry[:, dr, :], Xb[:, dr, b_, :], csc[:, qsl, 0, j:j + 1], None, ALU.mult,
                               ALU.bypass, [xb, CS], [RB("carry", dr)])
                        else:
                            cp("dve", carry[:, dr, :], Xb[:, dr, b_, :], [xb], [RB("carry", dr)])
                        dma("sp", st_d[seg, dr, hh], carry[:, dr, :], [RB("carry", dr)], [], ("stout", dr))
                yield

        def output_pass(hh):
            qsl = hh % 2
            for tg in range(3):
                t0, n, g = TGS[tg]
                ntile = n // 128
                i0 = t0 // 128
                for dr in range(2):
                    pt, pb = ps("o")
                    for ii in range(ntile):
                        i = i0 + ii
                        mm(pt[:, ii * 128:(ii + 1) * 128], qk[:, qsl, 2 * dr + 1, i * 128:(i + 1) * 128],
                           qk[:, qsl, 2 * dr, i * 128:(i + 1) * 128], True, True,
                           [RB("qk", qsl, 2 * dr + 1, tg), RB("qk", qsl, 2 * dr, tg)], [pb])
                    tt("dve", sct[:, dr, 0:n], pt[:, 0:n], (mfm if dr == 0 else mbm)[:, 0:n], ALU.mult,
                       [pb] + SM, [RB("sct", dr)])
                po, pob = ps("o")
                for ii in range(ntile):
                    i = i0 + ii
                    cs = slice(ii * 128, (ii + 1) * 128)
                    vl = vtok[:, i, hh * 128:(hh + 1) * 128]
                    mm(po[:, cs], vl, sct[:, 0, cs], True, False, [RB("vtok", i), RB("sct", 0)], [pob])
                    mm(po[:, cs], vl, sct[:, 1, cs], False, False, [RB("vtok", i), RB("sct", 1)], [pob])
                    for dr in range(2):
                        for hf in range(2):
                            j = i * 2 + hf
                            c2 = slice(ii * 128 + hf * 64, ii * 128 + (hf + 1) * 64)
                            mm(po[:, c2], sbf[:, dr, j, :], qk[:, qsl, 2 * dr, j * 64:(j + 1) * 64], False,
                               (dr == 1 and hf == 1), [RB("sbf", dr, j // 8), RB("qk", qsl, 2 * dr, tg)], [pob])
                act(sqb[:, 0, 0:n], po[:, 0:n], AF.Square, [pob], [B("sqb")])
                pt, pb = ps("o")
                mm(pt[:, 0:n], onesb[:], sqb[:, 0, 0:n], True, True, [B("sqb"), B("onesb")], [pb])
                act(rstd2[:, 0, 0:n], pt[:, 0:n], AF.Ln, [pb], [B("rstd", 0)], scale=1.0 / 128, bias=EPS)
                act(rstd2[:, 0, 0:n], rstd2[:, 0, 0:n], AF.Exp, [B("rstd", 0)], [B("rstd", 0)], scale=-0.5)
                tt("dve", ntmp[:, 0, 0:n], po[:, 0:n], rstd2[:, 0, 0:n], ALU.mult, [pob, B("rstd", 0)],
                   [B("ntmp", 0)])
                stt(oaT[:, hh, t0:t0 + n], ntmp[:, 0, 0:n], hng[:, hh:hh + 1], zogs[:, qsl, t0:t0 + n], ALU.mult,
                    ALU.mult, [B("ntmp", 0), RB("zogs", qsl, tg)] + SM, [B("oaT", hh, tg)])
                yield

        def run_all(*gens, reps=None):
            alive = [(g_, (reps[i] if reps else 1)) for i, g_ in enumerate(gens) if g_ is not None]
            while alive:
                for item in list(alive):
                    g_, r_ = item
                    for _ in range(r_):
                        try:
                            next(g_)
                        except StopIteration:
                            alive.remove(item)
                            break

        pools["st"] = [0, 1]
        pools["g"] = [2, 3, 4]
        pools["o"] = [5, 6, 7]
        pmode[0] = "split"
        def chain(*gens):
            for g_ in gens:
                yield from g_

        run_all(chain(gates(0), transposes(0)))
        for hh in range(4):
            run_all(chain(state_pass(hh), output_pass(hh)),
                    chain(gates(hh + 1), transposes(hh + 1)) if hh < 3 else None, reps=(1, 1))
        pmode[0] = "all"

        new_phase()
        tok_major(WB_POOL, zpool, "zpool")
        for g in range(4):
            sl = g % 2
            pa = pstr[:, sl, 0:8 * 1024].rearrange("p (a b) -> p a b", a=8)
            pbv = pstr[:, sl, 8 * 1024:8 * 1024 + 512].rearrange("p (a b) -> p a b", a=2)
            dma("sp", pa, poolA_d[g].rearrange("(st p) t -> p st t", p=128), [],
                [RB("pstr", sl, 0)] + fz("sp"), ("pstrA", sl))
            dma("sp", pbv, poolB_d[g].rearrange("(st p) t -> p st t", p=128), [], [RB("pstr", sl, 1)], ("pstrB", sl))
            p1 = []
            for tg in range(3):
                t0, n, gg = TGS[tg]
                dma("sp", invb[:, tg:tg + 1, 0:n], invc_d[g:g + 1, t0:t0 + n].partition_broadcast(128), [],
                    [RB("invb", tg)], ("invb", tg))
                pt, pb = ps()
                if tg < 2:
                    for st_ in range(8):
                        mm(pt[:, 0:n], zpool[:, st_, g * 128:(g + 1) * 128], pa[:, st_, t0:t0 + n], st_ == 0, st_ == 7,
                           [RB("zpool", st_), RB("pstr", sl, 0)], [pb])
                else:
                    for st_ in range(2):
                        mm(pt[:, 0:n], zpool[:, 8 + st_, g * 128:(g + 1) * 128], pbv[:, st_, :], st_ == 0, st_ == 1,
                           [RB("zpool", 8 + st_), RB("pstr", sl, 1)], [pb])
                p1.append((pt, pb))
            for tg in range(3):
                t0, n, gg = TGS[tg]
                tt("dve", pmT[:, tg, 0:n], p1[tg][0][:, 0:n], invb[:, tg, 0:n], ALU.mult, [p1[tg][1], RB("invb", tg)],
                   [RB("pmT", tg)])
            p2 = []
            for tg in range(3):
                t0, n, gg = TGS[tg]
                pt2, pb2 = ps()
                mm(pt2[:, 0:n], pwb[:, g, :], pmT[:, tg, 0:n], True, True, [B("pwb"), RB("pmT", tg)], [pb2])
                p2.append((pt2, pb2))
            for tg in range(3):
                t0, n, gg = TGS[tg]
                act(obT[:, g, t0:t0 + n], p2[tg][0][:, 0:n], AF.Copy, [p2[tg][1]] + SM, [B("obT", g, tg)],
                    scale=psc[:, g:g + 1])

        new_phase()
        for n_ in range(8):
            k = WB_ML[n_]
            s = wget(k)
            wga = rslice(s, 0, (KC, 128))
            wgb = rslice(s, KC * 128, (KC, 128))
            wa = rslice(s, 2 * KC * 128, (4, 128))
            wb = rslice(s, 2 * KC * 128 + 512, (4, 128))
            for tg in range(3):
                t0, n, g = TGS[tg]
                pga, pgab = feat_major(lambda kc, w=wga: w[:, kc, :], wbufs(k), tg)
                pgb, pgbb = feat_major(lambda kc, w=wgb: w[:, kc, :], wbufs(k), tg)
                pA, pAb = ps()
                for e in range(4):
                    mm(pA[:, 0:n], wa[:, e, :], oaT[:, e, t0:t0 + n], e == 0, e == 3, wbufs(k) + [B("oaT", e, tg)], [pAb])
                pB, pBb = ps()
                for e in range(4):
                    mm(pB[:, 0:n], wb[:, e, :], obT[:, e, t0:t0 + n], e == 0, e == 3, wbufs(k) + [B("obT", e, tg)], [pBb])
                act(sg32[:, 0, 0:n], pga[:, 0:n], AF.Sigmoid, [pgab], [RB("sg32", 0)] + fz("act"))
                act(sg32[:, 1, 0:n], pgb[:, 0:n], AF.Sigmoid, [pgbb], [RB("sg32", 1)])
                tt("dve", m1[:, 0, 0:n], pA[:, 0:n], sg32[:, 0, 0:n], ALU.mult, [pAb, RB("sg32", 0)],
                   [RB("m1", 0)] + fz("dve"))
                tt("dve", m1[:, 1, 0:n], pB[:, 0:n], sg32[:, 1, 0:n], ALU.mult, [pBb, RB("sg32", 1)], [RB("m1", 1)])
                tt("dve", mergedT[:, n_, t0:t0 + n], m1[:, 0, 0:n], m1[:, 1, 0:n], ALU.add,
                   [RB("m1", 0), RB("m1", 1)], [RB("mergedT", n_, tg)])
                it_ = n_ * 3 + tg
                mini_consume(16 + it_)
                if it_ == 23:
                    make_G(G2, g2c, 32, "2")

        def resid_proj(kbase, nblk, src, sname, nk, gate_c0, kcs_per_blk):
            for b in range(nblk):
                k = kbase + b
                s = wget(k)
                wv = rslice(s, 0, (nk, kcs_per_blk * 128))
                for jj in range(kcs_per_blk):
                    n2 = b * kcs_per_blk + jj
                    for tg in range(3):
                        t0, n, g = TGS[tg]
                        pt, pb = ps()
                        for kc in range(nk):
                            mm(pt[:, 0:n], wv[:, kc, jj * 128:(jj + 1) * 128], src[:, kc, t0:t0 + n], kc == 0,
                               kc == nk - 1, wbufs(k) + [RB(sname, kc, tg)], [pb])
                        stt(xT[:, n2, t0:t0 + n], pt[:, 0:n], modT[:, gate_c0 + n2, g:g + 1], xT[:, n2, t0:t0 + n],
                            ALU.mult, ALU.add, [pb, B("xT", n2, tg), MB(gate_c0), MB(gate_c0 + 4)], [B("xT", n2, tg)])

        resid_proj(WB_O, 2, mergedT, "mergedT", KC, 16, 4)
        norm_to_hT(G2, "2", 24)

        new_phase()
        for b in range(11):
            k = WB_FL[b]
            s = wget(k)
            for jj in range(2):
                j = 2 * b + jj
                wg = rslice(s, (2 * jj) * KC * 128, (KC, 128))
                wu = rslice(s, (2 * jj + 1) * KC * 128, (KC, 128))
                for tg in range(3):
                    t0, n, g = TGS[tg]
                    pg, pgb_ = feat_major(lambda kc, w=wg: w[:, kc, :], wbufs(k), tg)
                    pu, pub_ = feat_major(lambda kc, w=wu: w[:, kc, :], wbufs(k), tg)
                    sl = (j * 3 + tg) % 2
                    act(ntmp[:, sl, 0:n], pg[:, 0:n], AF.Silu, [pgb_], [B("ntmp", sl)])
                    tt("dve", actT[:, j, t0:t0 + n], pu[:, 0:n], ntmp[:, sl, 0:n], ALU.mult,
                       [pub_, B("ntmp", sl)], [RB("actT", j, tg)] + fz("dve"))
                    it_ = j * 3 + tg
                    if it_ < 8:
                        mini_consume(40 + it_)

        resid_proj(WB_FO, 8, actT, "actT", FC, 40, 1)

        yb_first = set()
        sq_eng[0] = "act"
        for tg in range(3):
            t0, n, g = TGS[tg]
            rs = rms_rstd(tg)
            for kc in range(KC):
                stt(xT[:, kc, t0:t0 + n], xT[:, kc, t0:t0 + n], fgc[:, kc:kc + 1], rstd2[:, rs, 0:n], ALU.mult,
                    ALU.mult, [B("xT", kc, tg), B("rstd", rs)] + SM, [B("xT", kc, tg)])
            LFall = [B("LF", a, b) for a in range(2) for b in range(2)]
            for ii in range(n // 128):
                i = t0 // 128 + ii
                s4 = i % 4
                stg, sl = (ytok, s4) if s4 < 2 else (ytokB, s4 - 2)
                sb_ = YB[sl] if s4 < 2 else [B("ytokB", sl, 0), B("ytokB", sl, 1)]
                for hf in range(2):
                    pt, pb = ps()
                    for j in range(4):
                        kc = hf * 4 + j
                        tr(pt[:, j * 128:(j + 1) * 128], xT[:, kc, i * 128:(i + 1) * 128], ident32[:],
                           [B("xT", kc, tg)] + SM, [pb])
                    wr = [sb_[hf]]
                    if s4 >= 2 and (sl, hf) not in yb_first:
                        yb_first.add((sl, hf))
                        wr = wr + LFall
                    cp("act" if hf == 0 else "dve", stg[:, sl, hf * 512:(hf + 1) * 512], pt[:, :], [pb], wr)
                dma("sp", y_d[i * 128:(i + 1) * 128, :], stg[:, sl, :], sb_, [], ("yout", s4))

        if debug:
            dma("sp", dbg_oaT, oaT[:], [B("oaT", a, b) for a in range(4) for b in range(3)], [], ("dbg",))
            dma("sp", dbg_obT, obT[:], [B("obT", a, b) for a in range(4) for b in range(3)], [], ("dbg",))
        fk = [("dbg",), ("yout", 0), ("yout", 1), ("yout", 2), ("yout", 3), ("stout", 0), ("stout", 1)]
        P.emit(nc, stack, [k_ for k_ in fk if k_ in P.dma_counts])
    return nc


_CACHE = {}


def kernel(x_prompt, x_sample, state_hgrn, c, c_ctx, w_ada, b_ada, norm1_g, w_in, hgrn_lb_logits, hgrn_norm_g,
           w_branch_a, pool_w, pool_scale, w_branch_b, w_out, norm2_g, w_ffn_in, w_ffn_out, final_g):
    f32 = lambda a: np.ascontiguousarray(np.asarray(a, dtype=np.float32))
    x_prompt, x_sample, state_hgrn, c, c_ctx = map(f32, (x_prompt, x_sample, state_hgrn, c, c_ctx))
    import os
    dbg = bool(os.environ.get("KDEBUG"))
    if "nc" not in _CACHE:
        _CACHE["nc"] = build_nc(debug=dbg)
        _CACHE["pc"] = _pool_consts()
        _CACHE["sc"] = _scan_consts()
    nc = _CACHE["nc"]
    p1, i1, p2, i2 = _CACHE["pc"]
    mf, mb, rmask = _CACHE["sc"]
    bf = ml_dtypes.bfloat16

    p1T = np.ascontiguousarray(p1.transpose(0, 2, 1))
    p2T = np.ascontiguousarray(p2.transpose(0, 2, 1))
    poolA_prompt = np.zeros((4, 1024, 1024), np.float32)
    for q in range(4):
        poolA_prompt[:, q * 256:(q + 1) * 256, q * 256:(q + 1) * 256] = p1T
    poolA_prompt = poolA_prompt.astype(bf)
    poolA_sample = p2T.astype(bf)
    poolB = p1T.astype(bf)
    inv_prompt = np.concatenate([np.tile(i1, (1, 4)), i1], axis=1)
    inv_sample = np.concatenate([i2, i1], axis=1)

    lbl = f32(hgrn_lb_logits)
    shared = {
        "mf": mf.astype(bf), "mb": mb.astype(bf), "rmask": rmask, "id32": np.eye(128, dtype=np.float32),
        "w_ada": f32(w_ada[0]), "b_adaT": _colT(b_ada[0], 48), "n1gT": _colT(norm1_g[0], 8),
        "n2gT": _colT(norm2_g[0], 8), "fgT": _colT(final_g, 8), "w_in": f32(w_in[0]),
        "lb0T": _colT(lbl[0].reshape(-1), 8), "lb1T": _colT(lbl[1].reshape(-1), 8),
        "hngT": _colT(hgrn_norm_g[0], 4), "wba": f32(w_branch_a[0]), "poolw": f32(pool_w[0]),
        "pscT": _colT(pool_scale[0], 4), "wbb": f32(w_branch_b[0]), "wout": f32(w_out[0]),
        "wffin": f32(w_ffn_in[0]), "wffout": f32(w_ffn_out[0]), "poolB": poolB,
    }
    in_maps = []
    plan = []
    for core in range(N_CORES):
        if core < 2:
            segs = [("s", core, q) for q in range(4)] + [("p", core, 0)]
            x = np.concatenate([x_sample[core], x_prompt[core]], axis=0)
            cv = np.stack([c[core], c_ctx])
            s0 = np.zeros((5, 2, 4, 128, 128), np.float32)
            s0[0, 0] = state_hgrn[core, 0, 0]
            s0[3, 1] = state_hgrn[core, 0, 1]
            flags = np.zeros((128, 10), np.float32)
            flags[:, 1:4] = 1.0
            flags[:, 5:8] = 1.0
            pa, inv = poolA_sample, inv_sample
        else:
            ids = [2 + (core - 2) * 5 + q for q in range(5)]
            segs = [("p", i, 0) for i in ids]
            x = np.concatenate([x_prompt[i] for i in ids], axis=0)
            cv = np.stack([c_ctx, c_ctx])
            s0 = np.zeros((5, 2, 4, 128, 128), np.float32)
            flags = np.zeros((128, 10), np.float32)
            pa, inv = poolA_prompt, inv_prompt
        plan.append(segs)
        cT = np.ascontiguousarray(cv.reshape(2, 8, 128).transpose(2, 0, 1).reshape(128, 16))
        m = dict(shared)
        m.update({"x": np.ascontiguousarray(x), "cT": cT, "s0": s0, "flags": flags, "poolA": pa,
                  "invc": np.ascontiguousarray(inv.astype(np.float32))})
        in_maps.append(m)

    res = run_bass_kernel_spmd(nc, in_maps, core_ids=list(range(N_CORES)))
    if dbg:
        _CACHE["dbg"] = [{k_: np.asarray(v_) for k_, v_ in r.items()} for r in res.results]
    y_prompt = np.zeros((32, 256, D), np.float32)
    y_sample = np.zeros((2, 1024, D), np.float32)
    new_state = np.zeros((32, 1, 2, 4, 128, 128), np.float32)
    for core in range(N_CORES):
        y = np.asarray(res.results[core]["y"], dtype=np.float32)
        st = np.asarray(res.results[core]["st"], dtype=np.float32)
        for sgi, (kind, idx, q) in enumerate(plan[core]):
            ys = y[sgi * 256:(sgi + 1) * 256]
            if kind == "s":
                y_sample[idx, q * 256:(q + 1) * 256] = ys
            else:
                y_prompt[idx] = ys
                new_state[idx, 0] = st[sgi]
    return (y_prompt, y_sample, new_state)
```

```python
import numpy as np
import ml_dtypes
import concourse.bass as bass
import concourse.mybir as mybir
from concourse.bass_utils import run_bass_kernel_spmd

F32 = mybir.dt.float32
BF16 = mybir.dt.bfloat16
AF = mybir.ActivationFunctionType
ALU = mybir.AluOpType

D = 1024
KC = 8
T = 1280
NT = 10
D_A = 512
D_FF = 2816
FC = 22
IN_COLS = 5120
EPS = 1e-6
CH = 64
NCH = 20
TGS = [(0, 512, 0), (512, 512, 0), (1024, 256, 1)]
N_CORES = 8
SLOT = 4096
NSLOT = 2
SYNC_SAME_ENGINE = True


class Buf:
    __slots__ = ("name", "w", "rs")

    def __init__(self, name):
        self.name = name
        self.w = None
        self.rs = []


class Strong:
    __slots__ = ("b",)

    def __init__(self, b):
        self.b = b


class Op:
    __slots__ = ("eng", "fn", "deps", "key", "sig", "tick", "dcount")

    def __init__(self, eng, fn, deps, key):
        self.eng = eng
        self.fn = fn
        self.deps = deps
        self.key = key
        self.sig = False
        self.tick = 0
        self.dcount = 0


class Prog:
    def __init__(self):
        self.ops = []
        self.dma_counts = {}

    def add(self, eng, fn, reads=(), writes=(), key=None):
        idx = len(self.ops)
        deps = {}
        strong = [b.b for b in writes if isinstance(b, Strong)]
        writes = [b.b if isinstance(b, Strong) else b for b in writes]
        for b in strong:
            if b.w is not None:
                deps[b.w] = "raw"
            for r in b.rs:
                deps[r] = "raw"

        def dep(i, kind):
            if i is None:
                return
            if i not in deps or kind == "raw":
                deps[i] = kind

        for b in reads:
            dep(b.w, "raw")
        for b in writes:
            dep(b.w, "waw")
            for r in b.rs:
                dep(r, "war")
        for b in reads:
            b.rs.append(idx)
        for b in writes:
            b.w = idx
            b.rs = []
        op = Op(eng, fn, deps, key)
        if key is not None:
            self.dma_counts[key] = self.dma_counts.get(key, 0) + 1
            op.dcount = self.dma_counts[key]
        self.ops.append(op)
        return idx

    @staticmethod
    def _needs(p, q, kind):
        if p.key is not None:
            return True
        if q.key is not None:
            return True
        if p.eng == q.eng and (p.eng == "pe" or (kind != "raw" and not SYNC_SAME_ENGINE)):
            return False
        return True

    def emit(self, nc, stack, final_keys):
        ops = self.ops
        for q in ops:
            best = {}
            for i, kind in q.deps.items():
                p = ops[i]
                if p.key is None and self._needs(p, q, kind):
                    if best.get(p.eng, -1) < i:
                        best[p.eng] = i
            q.deps = {i: k_ for i, k_ in q.deps.items() if ops[i].key is not None or best.get(ops[i].eng) == i}
            for i in best.values():
                ops[i].sig = True
        cnt = {}
        for p in ops:
            if p.sig:
                cnt[p.eng] = cnt.get(p.eng, 0) + 1
                p.tick = cnt[p.eng]
        eng_sem = {e: stack.enter_context(nc.semaphore("tick_" + e)) for e in ("pe", "act", "dve", "pool")}
        dma_sem = {k: stack.enter_context(nc.semaphore("dma_%s" % (k,))) for k in self.dma_counts}
        block = stack.enter_context(nc.Block())

        def run_engine(name, h):
            seen = {}
            for q in ops:
                if q.eng != name:
                    continue
                waits = {}
                for i, kind in q.deps.items():
                    p = ops[i]
                    if p.key is not None:
                        s, v = dma_sem[p.key], 16 * p.dcount
                    else:
                        if not self._needs(p, q, kind):
                            continue
                        s, v = eng_sem[p.eng], p.tick
                    if waits.get(s, (0,))[0] < v:
                        waits[s] = (v,)
                for s, (v,) in waits.items():
                    if seen.get(s, 0) >= v:
                        continue
                    h.wait_ge(s, v)
                    seen[s] = v
                ins = q.fn(h)
                if q.key is not None:
                    ins.then_inc(dma_sem[q.key], 16)
                elif q.sig:
                    ins.then_inc(eng_sem[q.eng], 1)
            if name == "sp":
                for k in final_keys:
                    h.wait_ge(dma_sem[k], 16 * self.dma_counts[k])

        @block.tensor
        def _(h):
            run_engine("pe", h)

        @block.scalar
        def _(h):
            run_engine("act", h)

        @block.vector
        def _(h):
            run_engine("dve", h)

        @block.gpsimd
        def _(h):
            run_engine("pool", h)

        @block.sync
        def _(h):
            run_engine("sp", h)


def _pool_consts():
    wins = (2, 4, 8, 16)
    p1 = np.zeros((4, 256, 256), np.float32)
    i1 = np.zeros((4, 256), np.float32)
    t = np.arange(256)
    for g, w in enumerate(wins):
        lo = np.clip(t - w // 2, 0, 255)
        hi = np.clip(t + w // 2 - 1, 0, 255)
        for tt in range(256):
            p1[g, tt, lo[tt]:hi[tt] + 1] = 1.0
            cnt = hi[tt] - lo[tt] + 1
            p1[g, tt, tt] -= cnt
            i1[g, tt] = 1.0 / cnt
    p2 = np.zeros((4, 1024, 1024), np.float32)
    i2 = np.zeros((4, 1024), np.float32)
    r = np.arange(16)
    c = np.arange(64)
    for g, w in enumerate(wins):
        rlo, rhi = np.clip(r - w // 2, 0, 15), np.clip(r + w // 2 - 1, 0, 15)
        clo, chi = np.clip(c - w // 2, 0, 63), np.clip(c + w // 2 - 1, 0, 63)
        blk = p2[g].reshape(16, 64, 16, 64)
        for rr in range(16):
            for cc in range(64):
                blk[rr, cc, rlo[rr]:rhi[rr] + 1, clo[cc]:chi[cc] + 1] = 1.0
                cnt = (rhi[rr] - rlo[rr] + 1) * (chi[cc] - clo[cc] + 1)
                blk[rr, cc, rr, cc] -= cnt
                i2[g, rr * 64 + cc] = 1.0 / cnt
    return p1, i1, p2, i2


def _scan_consts():
    s = np.arange(128)[:, None]
    t = np.arange(128)[None, :]
    same = (s // CH) == (t // CH)
    mf = (same & (s <= t)).astype(np.float32)
    mb = (same & (s >= t)).astype(np.float32)
    mf = np.tile(mf, (1, 4))
    mb = np.tile(mb, (1, 4))
    rmask = np.ones((128, 512), np.float32)
    rmask[:, ::CH] = 0.0
    return mf, mb, rmask


def _colT(v, n):
    return np.ascontiguousarray(np.asarray(v, np.float32).reshape(n, 128).T)


def build_nc(debug=False):
    from contextlib import ExitStack

    nc = bass.Bass("TRN2", target_bir_lowering=False)
    P = Prog()

    def din(name, shape, dt=F32):
        return nc.dram_tensor(name, list(shape), dt, kind="ExternalInput").ap()

    x_d = din("x", [T, D])
    cT_d = din("cT", [128, 16])
    s0_d = din("s0", [5, 2, 4, 128, 128])
    flags_d = din("flags", [128, 10])
    poolA_d = din("poolA", [4, 1024, 1024], BF16)
    poolB_d = din("poolB", [4, 256, 256], BF16)
    invc_d = din("invc", [4, T])
    mf_d = din("mf", [128, 512], BF16)
    mb_d = din("mb", [128, 512], BF16)
    rmask_d = din("rmask", [128, 512])
    id32_d = din("id32", [128, 128])
    w_ada_d = din("w_ada", [D, 6 * D])
    b_adaT_d = din("b_adaT", [128, 48])
    n1gT_d = din("n1gT", [128, 8])
    n2gT_d = din("n2gT", [128, 8])
    fgT_d = din("fgT", [128, 8])
    w_in_d = din("w_in", [D, IN_COLS])
    lb0T_d = din("lb0T", [128, 8])
    lb1T_d = din("lb1T", [128, 8])
    hngT_d = din("hngT", [128, 4])
    wba_d = din("wba", [D_A, D])
    poolw_d = din("poolw", [4, 128, 128])
    pscT_d = din("pscT", [128, 4])
    wbb_d = din("wbb", [D_A, D])
    wout_d = din("wout", [D, D])
    n_ffin_d = din("wffin", [D, 2 * D_FF])
    wffout_d = din("wffout", [D_FF, D])
    y_d = nc.dram_tensor("y", [T, D], F32, kind="ExternalOutput").ap()
    st_d = nc.dram_tensor("st", [5, 2, 4, 128, 128], F32, kind="ExternalOutput").ap()
    if debug:
        dbg_hT = nc.dram_tensor("dbg_hT", [128, KC, T], BF16, kind="ExternalOutput").ap()
        dbg_oaT = nc.dram_tensor("dbg_oaT", [128, 4, T], BF16, kind="ExternalOutput").ap()
        dbg_obT = nc.dram_tensor("dbg_obT", [128, 4, T], BF16, kind="ExternalOutput").ap()
        dbg_qk = nc.dram_tensor("dbg_qk", [128, 4, 4, T], BF16, kind="ExternalOutput").ap()
        dbg_v = nc.dram_tensor("dbg_v", [128, NT, 512], BF16, kind="ExternalOutput").ap()

    stack = ExitStack()
    with stack:
        ARENA = 212800
        arena = nc.alloc_sbuf_tensor("arena", [128, ARENA // 4], F32)
        base = nc.lookup_mloc(arena).addr
        off = [0]

        def alloc(name, shape, dt, at=None):
            nbytes = int(np.prod(shape[1:])) * (4 if dt == F32 else 2)
            nbytes = (nbytes + 31) // 32 * 32
            if at is None:
                at = off[0]
                off[0] += nbytes
                assert off[0] <= ARENA, (name, off[0])
            return nc.alloc_sbuf_tensor_at(name, list(shape), dt, offset=base + at), at, nbytes

        xT, _, _ = alloc("xT", [128, KC, T], F32)
        hT, _, _ = alloc("hT", [128, KC, T], BF16)
        ring, _, _ = alloc("ring", [128, NSLOT, SLOT], BF16)
        ident32, _, _ = alloc("ident32", [128, 128], F32)
        identb, _, _ = alloc("identb", [128, 128], BF16)
        onesb, _, _ = alloc("onesb", [128, 128], BF16)
        mfm, _, _ = alloc("mfm", [128, 512], BF16)
        mbm, _, _ = alloc("mbm", [128, 512], BF16)
        rmask, _, _ = alloc("rmask", [128, 512], F32)
        cT, _, _ = alloc("cT", [128, 16], F32)
        scT, _, _ = alloc("scT", [128, 2, 8], BF16)
        badaT, _, _ = alloc("badaT", [128, 48], F32)
        modT, _, _ = alloc("modT", [128, 48, 2], F32)
        g1c, _, _ = alloc("g1c", [128, 8], F32)
        g2c, _, _ = alloc("g2c", [128, 8], F32)
        fgc, _, _ = alloc("fgc", [128, 8], F32)
        G1, _, _ = alloc("G1", [128, 8, 2], F32)
        G2, _, _ = alloc("G2", [128, 8, 2], F32)
        lb0, _, _ = alloc("lb0", [128, 8], F32)
        lb1, _, _ = alloc("lb1", [128, 8], F32)
        lbc, _, _ = alloc("lbc", [128, 8], F32)
        omlb, _, _ = alloc("omlb", [128, 8], F32)
        nomlb, _, _ = alloc("nomlb", [128, 8], F32)
        hng, _, _ = alloc("hng", [128, 4], F32)
        psc, _, _ = alloc("psc", [128, 4], F32)
        flg, _, _ = alloc("flg", [128, 10], F32)
        pw32, _, _ = alloc("pw32", [128, 4, 128], F32)
        pwb, _, _ = alloc("pwb", [128, 4, 128], BF16)
        sqb, _, _ = alloc("sqb", [128, 4, 512], BF16)
        rstd2, _, _ = alloc("rstd", [128, 2, 512], F32)
        ntmp, _, _ = alloc("ntmp", [128, 2, 512], F32)
        oaT, _, _ = alloc("oaT", [128, 4, T], BF16)
        obT, obo, _ = alloc("obT", [128, 4, T], BF16)
        ktokB = nc.alloc_sbuf_tensor_at("ktokB", [128, 2, 2, NT, 128], BF16, offset=base + obo)
        R0 = off[0]
        qk, _, _ = alloc("qk", [128, 2, 4, T], BF16)
        vtok, _, _ = alloc("vtok", [128, NT, 512], BF16)
        zogs, _, _ = alloc("zogs", [128, 2, T], BF16)
        ktok, _, _ = alloc("ktok", [128, 2, 2, NT, 128], BF16)
        sbf, _, _ = alloc("sbf", [128, 2, NCH, 128], BF16)
        assert off[0] - R0 == 56320, off[0] - R0
        csc, _, _ = alloc("csc", [128, 2, 2, NCH], F32)
        ctmp, _, _ = alloc("ctmp", [128, 8], F32)
        Xb, _, _ = alloc("Xb", [128, 2, 2, 128], F32)
        carry, _, _ = alloc("carry", [128, 2, 128], F32)
        s0t, _, _ = alloc("s0t", [128, 2, 2, 128], F32)
        sct, _, _ = alloc("sct", [128, 2, 512], BF16)
        off[0] = max(off[0], R0 + FC * T * 2)
        q32, q32o, _ = alloc("q32", [128, 2, 512], F32)
        minir = nc.alloc_sbuf_tensor_at("minir", [128, 2, KC * 128], BF16, offset=base + q32o)
        LFt, lfo, _ = alloc("LFt", [128, 2, 2, 520], F32)
        KKt, _, _ = alloc("KKt", [128, 2, 2, 512], BF16)
        SGt, g4o, _ = alloc("SGt", [128, 2, 2, 512], F32)
        BBt, _, _ = alloc("BBt", [128, 2, 512], F32)
        ytok = nc.alloc_sbuf_tensor_at("ytok", [128, 2, D], F32, offset=base + g4o)
        ytokB = nc.alloc_sbuf_tensor_at("ytokB", [128, 2, D], F32, offset=base + lfo)
        assert off[0] <= ARENA, off[0]
        PSL = 8 * 1024 + 2 * 256
        pstr = nc.alloc_sbuf_tensor_at("pstr", [128, 2, PSL], BF16, offset=base + R0)
        zpool = nc.alloc_sbuf_tensor_at("zpool", [128, NT, 512], BF16, offset=base + R0 + 34816)
        invb = nc.alloc_sbuf_tensor_at("invb", [128, 3, 512], F32, offset=base + R0 + 45056)
        pmT = nc.alloc_sbuf_tensor_at("pmT", [128, 3, 512], BF16, offset=base + R0 + 51200)
        assert 51200 + 3072 <= 56320
        mergedT = nc.alloc_sbuf_tensor_at("mergedT", [128, KC, T], BF16, offset=base + R0)
        sg32 = nc.alloc_sbuf_tensor_at("sg32", [128, 2, 512], F32, offset=base + R0 + 20480)
        m1 = nc.alloc_sbuf_tensor_at("m1", [128, 2, 512], F32, offset=base + R0 + 24576)
        actT = nc.alloc_sbuf_tensor_at("actT", [128, FC, T], BF16, offset=base + R0)

        banks = [nc.alloc_psum_tensor("ps%d" % i, [128, 512], F32) for i in range(8)]
        bbuf = [Buf("ps%d" % i) for i in range(8)]
        bctr = {}
        pools = {"all": list(range(8))}
        pmode = ["all"]

        def ps(pool=None):
            name = pool if (pool is not None and pmode[0] == "split") else "all"
            lst = pools[name]
            c = bctr.get(name, 0)
            bctr[name] = c + 1
            i = lst[c % len(lst)]
            return banks[i], bbuf[i]

        bufs = {}
        REG = []

        def B(*k):
            if k not in bufs:
                bufs[k] = Buf(str(k))
            return bufs[k]

        def RB(*k):
            if k not in bufs:
                bufs[k] = Buf(str(k))
                REG.append(bufs[k])
            return bufs[k]

        fence = {"act": [], "dve": [], "sp": []}

        def new_phase():
            for e in fence:
                fence[e] = list(REG)

        def fz(eng):
            r = [Strong(b_) for b_ in fence[eng]]
            fence[eng] = []
            return r

        def dma(eng, out, in_, reads, writes, key):
            P.add(eng, lambda h, o=out, i=in_: h.dma_start(out=o, in_=i), reads, writes, key)

        def act(out, in_, func, reads, writes, scale=1.0, bias=0.0):
            P.add("act", lambda h, o=out, i=in_, f=func, s=scale, b=bias: h.activation(o, i, f, bias=b, scale=s),
                  reads, writes)

        def tt(eng, out, a, b, op, reads, writes):
            P.add(eng, lambda h, o=out, a=a, b=b, op=op: h.tensor_tensor(o, a, b, op), reads, writes)

        def ts(eng, out, a, s1, s2, op0, op1, reads, writes):
            P.add(eng, lambda h, o=out, a=a, s1=s1, s2=s2, op0=op0, op1=op1: h.tensor_scalar(o, a, s1, s2, op0, op1),
                  reads, writes)

        def stt(out, a, s, b, op0, op1, reads, writes):
            P.add("dve", lambda h, o=out, a=a, s=s, b=b, op0=op0, op1=op1: h.scalar_tensor_tensor(o, a, s, b, op0, op1),
                  reads, writes)

        def cp(eng, out, in_, reads, writes):
            if eng == "act":
                P.add("act", lambda h, o=out, i=in_: h.copy(o, i), reads, writes)
            else:
                P.add(eng, lambda h, o=out, i=in_: h.tensor_copy(o, i), reads, writes)

        def mm(out, lhsT, rhs, start, stop, reads, writes):
            P.add("pe", lambda h, o=out, l=lhsT, r=rhs, s=start, e=stop: h.matmul(o, l, r, start=s, stop=e),
                  reads, writes)

        def tr(out, in_, ident, reads, writes):
            P.add("pe", lambda h, o=out, i=in_, d=ident: h.transpose(o, i, d), reads, writes)

        wblocks = []

        def wview(w_d, c0, n):
            return w_d.rearrange("(kc p) n -> p kc n", p=128)[:, :, c0:c0 + n]

        wstate = {"issued": 0}

        def slot_bufs(s):
            return [B("ring", s, pi) for pi in range(4)]

        def w_issue_upto(k):
            while wstate["issued"] <= min(k, len(wblocks) - 1):
                j = wstate["issued"]
                s = j % NSLOT
                pieces = wblocks[j]
                for pi, (o, shp, src) in enumerate(pieces):
                    n_el = int(np.prod(shp))
                    dst = ring[:, s, o:o + n_el].rearrange("p (a b) -> p a b", a=shp[0])
                    wr = slot_bufs(s) if len(pieces) == 1 else [B("ring", s, pi)]
                    dma("pool", dst, src, [], wr, ("ring", s, pi))
                wstate["issued"] += 1

        def wget(k):
            w_issue_upto(k + NSLOT - 1)
            return k % NSLOT

        def wbufs(k):
            return slot_bufs(k % NSLOT)

        WB_A = {}

        def ada_blk(b):
            WB_A[b] = len(wblocks)
            wblocks.append([(0, (KC, 512), wview(w_ada_d, b * 512, 512))])

        for b in range(4):
            ada_blk(b)
        WB_V = len(wblocks)
        wblocks.append([(0, (KC, 512), wview(w_in_d, 1536, 512))])
        WB_HL = []
        for h_ in range(4):
            WB_HL.append(len(wblocks))
            wblocks.append([(j * KC * 128, (KC, 128), wview(w_in_d, c_ + h_ * 128, 128))
                            for j, c_ in enumerate((0, 512, 1024, 2048))])
        WB_POOL = len(wblocks)
        wblocks.append([(0, (KC, 512), wview(w_in_d, 2560, 512))])
        WB_ML = []
        for n_ in range(8):
            WB_ML.append(len(wblocks))
            wblocks.append([
                (0, (KC, 128), wview(w_in_d, 3072 + n_ * 128, 128)),
                (KC * 128, (KC, 128), wview(w_in_d, 4096 + n_ * 128, 128)),
                (2 * KC * 128, (4, 128), wview(wba_d, n_ * 128, 128)),
                (2 * KC * 128 + 512, (4, 128), wview(wbb_d, n_ * 128, 128)),
            ])
        WB_O = len(wblocks)
        for b in range(2):
            wblocks.append([(0, (KC, 512), wview(wout_d, b * 512, 512))])
        WB_FL = []
        for b in range(11):
            WB_FL.append(len(wblocks))
            wblocks.append([
                (0, (KC, 128), wview(n_ffin_d, (2 * b) * 128, 128)),
                (KC * 128, (KC, 128), wview(n_ffin_d, D_FF + (2 * b) * 128, 128)),
                (2 * KC * 128, (KC, 128), wview(n_ffin_d, (2 * b + 1) * 128, 128)),
                (3 * KC * 128, (KC, 128), wview(n_ffin_d, D_FF + (2 * b + 1) * 128, 128)),
            ])
        WB_FO = len(wblocks)
        for b in range(8):
            wblocks.append([(0, (FC, 128), wview(wffout_d, b * 128, 128))])

        def rslice(s, o, shp):
            n_el = int(np.prod(shp))
            return ring[:, s, o:o + n_el].rearrange("p (a b) -> p a b", a=shp[0])

        def tg_of_tile(i):
            return 0 if i < 4 else (1 if i < 8 else 2)

        YB = [[B("SG", 0, 0), B("SG", 0, 1)], [B("SG", 1, 0), B("SG", 1, 1)]]
        small = [(ident32, id32_d), (cT, cT_d), (badaT, b_adaT_d), (g1c, n1gT_d), (mfm, mf_d), (mbm, mb_d),
                 (rmask, rmask_d), (g2c, n2gT_d), (fgc, fgT_d), (lb0, lb0T_d), (lb1, lb1T_d), (hng, hngT_d),
                 (psc, pscT_d), (flg, flags_d)]
        SM = [B("small", tns.name) for tns, _ in small] + [B("small", "pw32")]

        def x_load(i):
            sl = i % 2
            dma("sp", ytok[:, sl, :], x_d[i * 128:(i + 1) * 128, :], [], YB[sl], ("ytok", sl))

        dma("sp", ident32[:], id32_d, [], [B("small", "ident32")], ("small",))
        x_load(0)
        x_load(1)
        for tns, src_ in small[1:]:
            dma("sp", tns[:], src_, [], [B("small", tns.name)], ("small",))
        dma("sp", pw32[:], poolw_d.rearrange("g c d -> c g d"), [], [B("small", "pw32")], ("small",))

        P.add("pool", lambda h: h.memset(ktok[:].rearrange("p a b c d -> p (a b c d)"), 0.0), [], [RB("ktokz", 0)])
        P.add("pool", lambda h: h.memset(ktokB[:].rearrange("p a b c d -> p (a b c d)"), 0.0), [],
              [RB("ktokz", 1)])
        P.add("dve", lambda h: h.tensor_copy(identb[:], ident32[:]), SM, [B("identb")])
        P.add("dve", lambda h: h.memset(onesb[:], 1.0), [], [B("onesb")])
        P.add("dve", lambda h: h.tensor_copy(pwb[:], pw32[:]), SM, [B("pwb")])
        P.add("dve", lambda h: h.memset(LFt[:, :, :, 0:8], 0.0), [],
              [B("LF", a, b) for a in range(2) for b in range(2)])
        tt("dve", lb1[:], lb0[:], lb1[:], ALU.subtract, SM, [B("lbd")])
        act(lbc[:], lb1[:], AF.Sigmoid, [B("lbd")], [B("lbc")])
        ts("dve", omlb[:], lbc[:], -1.0, 1.0, ALU.mult, ALU.add, [B("lbc")], [B("omlb")])
        ts("dve", nomlb[:], lbc[:], 1.0, -1.0, ALU.mult, ALU.add, [B("lbc")], [B("omlb")])
        act(scT[:].rearrange("p g k -> p (g k)"), cT[:], AF.Silu, SM, [B("scT")])

        def x_tile(i):
            sl = i % 2
            for hf in range(2):
                pt, pb = ps()
                for j in range(4):
                    kc = hf * 4 + j
                    tr(pt[:, j * 128:(j + 1) * 128], ytok[:, sl, kc * 128:(kc + 1) * 128], ident32[:],
                       YB[sl] + SM, [pb])
                cp("dve" if hf == 0 else "act", xT[:, hf * 4:hf * 4 + 4, i * 128:(i + 1) * 128],
                   pt[:, :].rearrange("p (a b) -> p a b", a=4), [pb],
                   [B("xT", hf * 4 + j, tg_of_tile(i)) for j in range(4)])
            if i + 2 < NT:
                x_load(i + 2)

        def MB(c):
            return B("modblk", c // 4)

        def adaln_block(bb):
            k = WB_A[bb]
            s = wget(k)
            wv = rslice(s, 0, (KC, 512))
            pt, pb = ps()
            for j in range(4):
                for kc in range(KC):
                    mm(pt[:, 2 * j:2 * j + 2], wv[:, kc, j * 128:(j + 1) * 128], scT[:, :, kc],
                       kc == 0, kc == KC - 1, wbufs(k) + [B("scT")], [pb])
            tt("dve", modT[:, bb * 4:bb * 4 + 4, :], pt[:, 0:8].rearrange("p (c g) -> p c g", g=2),
               badaT[:, bb * 4:bb * 4 + 4].unsqueeze(2).broadcast_to([128, 4, 2]), ALU.add,
               [pb] + SM, [B("modblk", bb)])

        mini = {"next": 16}

        def mini_issue_upto(c):
            while mini["next"] <= min(c, 47):
                cc = mini["next"]
                sl_ = cc % 2
                dst = minir[:, sl_, :].rearrange("p (a b) -> p a b", a=KC)
                dma("pool", dst, wview(w_ada_d, cc * 128, 128), [], [B("q32", sl_)], ("mini", sl_))
                mini["next"] += 1

        def mini_consume(c):
            mini_issue_upto(c + 1)
            sl_ = c % 2
            pt, pb = ps()
            for kc in range(KC):
                mm(pt[:, 0:2], minir[:, sl_, kc * 128:(kc + 1) * 128], scT[:, :, kc], kc == 0, kc == KC - 1,
                   [B("q32", sl_), B("scT")], [pb])
            ts("dve", modT[:, c, :], pt[:, 0:2], badaT[:, c:c + 1], None, ALU.add, ALU.bypass, [pb] + SM, [MB(c)])

        def make_G(Gt, gcol, sc0, name):
            for g in range(2):
                stt(Gt[:, :, g], modT[:, sc0:sc0 + 8, g], 1.0, gcol[:], ALU.add, ALU.mult,
                    [MB(sc0), MB(sc0 + 4)] + SM, [B("G", name)])

        sq_eng = ["act"]

        def rms_rstd(tg, slot=None):
            t0, n, g = TGS[tg]
            pt, pb = ps()
            for hf in range(2):
                if sq_eng[0] == "act":
                    act(sqb[:, :, 0:n], xT[:, hf * 4:hf * 4 + 4, t0:t0 + n], AF.Square,
                        [B("xT", hf * 4 + j, tg) for j in range(4)], [B("sqb")])
                else:
                    tt("pool", sqb[:, :, 0:n], xT[:, hf * 4:hf * 4 + 4, t0:t0 + n],
                       xT[:, hf * 4:hf * 4 + 4, t0:t0 + n], ALU.mult,
                       [B("xT", hf * 4 + j, tg) for j in range(4)], [B("sqb")])
                for j in range(4):
                    mm(pt[:, 0:n], onesb[:], sqb[:, j, 0:n], hf == 0 and j == 0, hf == 1 and j == 3,
                       [B("sqb"), B("onesb")], [pb])
            rs = tg % 2 if slot is None else slot
            act(rstd2[:, rs, 0:n], pt[:, 0:n], AF.Ln, [pb], [B("rstd", rs)], scale=1.0 / D, bias=EPS)
            act(rstd2[:, rs, 0:n], rstd2[:, rs, 0:n], AF.Exp, [B("rstd", rs)], [B("rstd", rs)], scale=-0.5)
            return rs

        def modulate(Gt, name, shift_c0, tg, rs):
            t0, n, g = TGS[tg]
            for kc in range(KC):
                sl = kc % 2
                stt(ntmp[:, sl, 0:n], xT[:, kc, t0:t0 + n], Gt[:, kc, g:g + 1], rstd2[:, rs, 0:n], ALU.mult, ALU.mult,
                    [B("xT", kc, tg), B("rstd", rs), B("G", name)], [B("ntmp", sl)])
                act(hT[:, kc, t0:t0 + n], ntmp[:, sl, 0:n], AF.Identity,
                    [B("ntmp", sl), MB(shift_c0), MB(shift_c0 + 4)], [B("hT", kc, tg)],
                    bias=modT[:, shift_c0 + kc, g:g + 1])

        def norm_tg(Gt, name, shift_c0, tg):
            rs = rms_rstd(tg)
            modulate(Gt, name, shift_c0, tg, rs)

        def norm_to_hT(Gt, name, shift_c0):
            for tg in range(3):
                norm_tg(Gt, name, shift_c0, tg)

        x_tile(0)
        x_tile(1)
        adaln_block(0)
        x_tile(2)
        x_tile(3)
        r0 = rms_rstd(0, 0)
        adaln_block(1)
        x_tile(4)
        x_tile(5)
        adaln_block(2)
        x_tile(6)
        x_tile(7)
        r1 = rms_rstd(1, 1)
        adaln_block(3)
        make_G(G1, g1c, 8, "1")
        modulate(G1, "1", 0, 0, r0)
        x_tile(8)
        x_tile(9)
        r2 = rms_rstd(2, 0)
        modulate(G1, "1", 0, 1, r1)
        modulate(G1, "1", 0, 2, r2)
        sq_eng[0] = "pool"
        if debug:
            dma("sp", dbg_hT, hT[:], [B("hT", kc, tg) for kc in range(KC) for tg in range(3)], [], ("dbg",))

        def hT_bufs(tg):
            return [B("hT", kc, tg) for kc in range(KC)]

        def tok_major(k, dst, dname, tiles=None):
            s = wget(k)
            wv = rslice(s, 0, (KC, 512))
            for i in (range(NT) if tiles is None else tiles):
                pt, pb = ps()
                for kc in range(KC):
                    mm(pt[:, :], hT[:, kc, i * 128:(i + 1) * 128], wv[:, kc, :], kc == 0, kc == KC - 1,
                       wbufs(k) + hT_bufs(tg_of_tile(i)), [pb])
                eng = "dve" if i % 2 == 0 else "act"
                cp(eng, dst[:, i, :], pt[:, :], [pb], [RB(dname, i)] + fz(eng))

        def feat_major(wv_fn, rbufs, tg, pool=None):
            t0, n, g = TGS[tg]
            pt, pb = ps(pool)
            for kc in range(KC):
                mm(pt[:, 0:n], wv_fn(kc), hT[:, kc, t0:t0 + n], kc == 0, kc == KC - 1, rbufs + hT_bufs(tg), [pb])
            return pt, pb

        new_phase()
        tok_major(WB_V, vtok, "vtok")

        qctr = [0]

        def gates(hh):
            k = WB_HL[hh]
            s = wget(k)
            qsl = hh % 2
            CS = RB("csc", qsl)
            wo = rslice(s, 3 * KC * 128, (KC, 128))
            zp = [feat_major(lambda kc, w=wo: w[:, kc, :], wbufs(k), tg, "g") for tg in range(3)]
            for tg in range(3):
                t0, n, g = TGS[tg]
                act(zogs[:, qsl, t0:t0 + n], zp[tg][0][:, 0:n], AF.Silu, [zp[tg][1]],
                    [RB("zogs", qsl, tg)] + fz("act"))
            yield
            wq = rslice(s, 0, (KC, 128))
            wfs = [rslice(s, (1 + dr) * KC * 128, (KC, 128)) for dr in range(2)]

            def proj(tg, which):
                w = wq if which == 0 else wfs[which - 1]
                return feat_major(lambda kc, w=w: w[:, kc, :], wbufs(k), tg, "g")

            nxt = [proj(0, 0), proj(0, 1), proj(0, 2)]
            yield
            for tg in range(3):
                t0, n, g = TGS[tg]
                nch = n // CH
                c0 = t0 // CH
                qs = qctr[0] % 2
                qctr[0] += 1
                curp = nxt
                nxt = []
                cp("act", q32[:, qs, 0:n], curp[0][0][:, 0:n], [curp[0][1]], [B("q32", qs)])
                pf = [curp[1], curp[2]]
                pr = qs
                SGv = [SGt[:, pr, dr, 0:n] for dr in range(2)]
                LFv = [LFt[:, pr, dr, 8:8 + n] for dr in range(2)]
                LFs = [LFt[:, pr, dr, 7:7 + n] for dr in range(2)]
                BBv = [BBt[:, dr, 0:n] for dr in range(2)]
                KKv = [KKt[:, pr, dr, 0:n] for dr in range(2)]
                bSG = [B("SG", pr, dr) for dr in range(2)]
                bLF = [B("LF", pr, dr) for dr in range(2)]
                bKK = [B("KK", pr, dr) for dr in range(2)]
                li = [dr * 4 + hh for dr in range(2)]
                for dr in range(2):
                    act(SGv[dr], pf[dr][0][:, 0:n], AF.Sigmoid, [pf[dr][1]], [bSG[dr]])
                yield
                if tg < 2:
                    nxt.append(proj(tg + 1, 0))
                for dr in range(2):
                    act(LFv[dr], SGv[dr], AF.Ln, [bSG[dr], B("omlb"), B("lbc")], [bLF[dr]],
                        scale=omlb[:, li[dr]:li[dr] + 1], bias=lbc[:, li[dr]:li[dr] + 1])
                    ts("pool", KKv[dr], SGv[dr], nomlb[:, li[dr]:li[dr] + 1], omlb[:, li[dr]:li[dr] + 1], ALU.mult,
                       ALU.add, [bSG[dr], B("omlb")], [bKK[dr]])
                yield
                if tg < 2:
                    nxt.append(proj(tg + 1, 1))
                P.add("dve", lambda h, n=n, o=BBv[0], l=LFv[0]: h.tensor_tensor_scan(o, rmask[:, 0:n], l, 0.0,
                                                                                     ALU.mult, ALU.add),
                      [bLF[0]] + SM, [B("BB", 0)])
                P.add("dve", lambda h, n=n, o=BBv[1], l=LFs[1]: h.tensor_tensor_scan(o, l, rmask[:, 0:n], 0.0,
                                                                                     ALU.add, ALU.mult),
                      [bLF[1]] + SM, [B("BB", 1)])
                yield
                if tg < 2:
                    nxt.append(proj(tg + 1, 2))
                b3 = [BBv[dr].rearrange("p (c t) -> p c t", t=CH) for dr in range(2)]
                l3 = LFv[1].rearrange("p (c t) -> p c t", t=CH)
                act(csc[:, qsl, 0, c0:c0 + nch], b3[0][:, :, 63], AF.Exp, [B("BB", 0)], [CS])
                tt("dve", ctmp[:, 0:nch], b3[1][:, :, 63], l3[:, :, 63], ALU.add, [B("BB", 1), bLF[1]],
                   [RB("ctmp")])
                act(csc[:, qsl, 1, c0:c0 + nch], ctmp[:, 0:nch], AF.Exp, [RB("ctmp")], [CS])
                for dr in range(2):
                    sg = 1.0 if dr == 0 else -1.0
                    act(SGv[dr], BBv[dr], AF.Exp, [B("BB", dr)], [bSG[dr]], scale=sg)
                yield
                for dr in range(2):
                    sg = -1.0 if dr == 0 else 1.0
                    act(LFv[dr], BBv[dr], AF.Exp, [B("BB", dr)], [bLF[dr]], scale=sg)
                    tt("pool", qk[:, qsl, 2 * dr, t0:t0 + n], q32[:, qs, 0:n], SGv[dr], ALU.mult,
                       [B("q32", qs), bSG[dr]], [RB("qk", qsl, 2 * dr, tg)])
                yield
                for dr in range(2):
                    tt("pool", qk[:, qsl, 2 * dr + 1, t0:t0 + n], KKv[dr], LFv[dr], ALU.mult,
                       [bKK[dr], bLF[dr]], [RB("qk", qsl, 2 * dr + 1, tg)])
                yield

        def transposes(hh):
            qsl = hh % 2
            kt = ktok if qsl == 0 else ktokB
            if debug:
                dma("sp", dbg_qk[:, hh], qk[:, qsl], [RB("qk", qsl, a, b) for a in range(4) for b in range(3)], [],
                    ("dbg",))
                if hh == 0:
                    dma("sp", dbg_v, vtok[:], [RB("vtok", i) for i in range(NT)], [], ("dbg",))
            for dr in range(2):
                for i0 in (0, 8):
                    nt_ = min(8, NT - i0)
                    pt, pb = ps("o")
                    ptb = pt.bitcast(BF16)
                    for i in range(i0, i0 + nt_):
                        tr(ptb[:, (i - i0) * 128:(i - i0 + 1) * 128], qk[:, qsl, 2 * dr + 1, i * 128:(i + 1) * 128],
                           identb[:], [RB("qk", qsl, 2 * dr + 1, tg_of_tile(i)), B("identb")], [pb])
                    eng = "act" if dr == 0 else "dve"
                    for hf in range(2):
                        rs = slice(hf * 64, (hf + 1) * 64)
                        cp(eng, kt[rs, hf, dr, i0:i0 + nt_, :],
                           ptb[rs, 0:nt_ * 128].rearrange("p (a b) -> p a b", a=nt_), [pb, RB("ktokz", qsl)],
                           [RB("ktok", qsl, dr, i0)])
                    yield

        S_init_done = [False, False]
        cur = [0, 0]
        s0cnt = [0, 0]
        s0slot = {}

        def s0_issue(hh, dr, seg):
            if hh > 3 or (hh, dr, seg) in s0slot:
                return
            ssl = s0cnt[dr] % 2
            s0cnt[dr] += 1
            s0slot[(hh, dr, seg)] = ssl
            dma("sp", s0t[:, dr, ssl, :], s0_d[seg, dr, hh], [], [RB("s0t", dr, ssl)], ("s0t", dr, ssl))

        def s0_next(hh, dr, seg):
            if dr == 0:
                return (hh, 0, seg + 1) if seg < 4 else (hh + 1, 0, 0)
            return (hh, 1, seg - 1) if seg > 0 else (hh + 1, 1, 4)

        def state_pass(hh):
            qsl = hh % 2
            kt = ktok if qsl == 0 else ktokB
            CS = RB("csc", qsl)
            for step in range(NCH):
                for dr in range(2):
                    j = step if dr == 0 else NCH - 1 - step
                    seg = j // 4
                    first = (j % 4 == 0) if dr == 0 else (j % 4 == 3)
                    last = (j % 4 == 3) if dr == 0 else (j % 4 == 0)
                    i, hf = j // 2, j % 2
                    xb = RB("X", dr)
                    a_, b_ = cur[dr], 1 - cur[dr]
                    if first:
                        s0_issue(hh, dr, seg)
                        ssl = s0slot[(hh, dr, seg)]
                        if not S_init_done[dr]:
                            P.add("dve", lambda h, dr=dr: h.memset(carry[:, dr, :], 0.0), [], [RB("carry", dr)])
                            S_init_done[dr] = True
                        fcol = (seg if dr == 0 else 5 + seg)
                        stt(Xb[:, dr, a_, :], carry[:, dr, :], flg[:, fcol:fcol + 1], s0t[:, dr, ssl, :],
                            ALU.mult, ALU.add, [RB("carry", dr), RB("s0t", dr, ssl)] + SM, [xb])
                        s0_issue(*s0_next(hh, dr, seg))
                    pt, pb = ps("st")
                    mm(pt[:, 0:128], kt[:, hf, dr, i, :], vtok[:, i, hh * 128:(hh + 1) * 128], True, True,
                       [RB("ktok", qsl, dr, 0 if i < 8 else 8), RB("vtok", i), RB("ktokz", qsl)], [pb])
                    if dr == 0 and first:
                        cj = 1.0
                    elif dr == 0:
                        cj = csc[:, qsl, 0, j - 1:j]
                    else:
                        cj = csc[:, qsl, 1, j:j + 1]
                    ts("pool", sbf[:, dr, j, :], Xb[:, dr, a_, :], cj, 0.0, ALU.mult, ALU.add, [xb, CS],
                       [RB("sbf", dr, j // 8)])
                    stt(Xb[:, dr, b_, :], Xb[:, dr, a_, :], cj, pt[:, 0:128], ALU.mult, ALU.add, [xb, pb, CS], [xb])
                    cur[dr] = b_
                    if last:
                        if dr == 0:
                            ts("dve", carry[:, dr, :], Xb[:, dr, b_, :], csc[:, qsl, 0, j:j + 1], None, ALU.mult,
                               ALU.bypass, [xb, CS], [RB("carry", dr)])
                        else:
                            cp("dve", carry[:, dr, :], Xb[:, dr, b_, :], [xb], [RB("carry", dr)])
                        dma("sp", st_d[seg, dr, hh], carry[:, dr, :], [RB("carry", dr)], [], ("stout", dr))
                yield

        def output_pass(hh):
            qsl = hh % 2
            for tg in range(3):
                t0, n, g = TGS[tg]
                ntile = n // 128
                i0 = t0 // 128
                for dr in range(2):
                    pt, pb = ps("o")
                    for ii in range(ntile):
                        i = i0 + ii
                        mm(pt[:, ii * 128:(ii + 1) * 128], qk[:, qsl, 2 * dr + 1, i * 128:(i + 1) * 128],
                           qk[:, qsl, 2 * dr, i * 128:(i + 1) * 128], True, True,
                           [RB("qk", qsl, 2 * dr + 1, tg), RB("qk", qsl, 2 * dr, tg)], [pb])
                    tt("dve", sct[:, dr, 0:n], pt[:, 0:n], (mfm if dr == 0 else mbm)[:, 0:n], ALU.mult,
                       [pb] + SM, [RB("sct", dr)])
                po, pob = ps("o")
                for ii in range(ntile):
                    i = i0 + ii
                    cs = slice(ii * 128, (ii + 1) * 128)
                    vl = vtok[:, i, hh * 128:(hh + 1) * 128]
                    mm(po[:, cs], vl, sct[:, 0, cs], True, False, [RB("vtok", i), RB("sct", 0)], [pob])
                    mm(po[:, cs], vl, sct[:, 1, cs], False, False, [RB("vtok", i), RB("sct", 1)], [pob])
                    for dr in range(2):
                        for hf in range(2):
                            j = i * 2 + hf
                            c2 = slice(ii * 128 + hf * 64, ii * 128 + (hf + 1) * 64)
                            mm(po[:, c2], sbf[:, dr, j, :], qk[:, qsl, 2 * dr, j * 64:(j + 1) * 64], False,
                               (dr == 1 and hf == 1), [RB("sbf", dr, j // 8), RB("qk", qsl, 2 * dr, tg)], [pob])
                act(sqb[:, 0, 0:n], po[:, 0:n], AF.Square, [pob], [B("sqb")])
                pt, pb = ps("o")
                mm(pt[:, 0:n], onesb[:], sqb[:, 0, 0:n], True, True, [B("sqb"), B("onesb")], [pb])
                act(rstd2[:, 0, 0:n], pt[:, 0:n], AF.Ln, [pb], [B("rstd", 0)], scale=1.0 / 128, bias=EPS)
                act(rstd2[:, 0, 0:n], rstd2[:, 0, 0:n], AF.Exp, [B("rstd", 0)], [B("rstd", 0)], scale=-0.5)
                tt("dve", ntmp[:, 0, 0:n], po[:, 0:n], rstd2[:, 0, 0:n], ALU.mult, [pob, B("rstd", 0)],
                   [B("ntmp", 0)])
                stt(oaT[:, hh, t0:t0 + n], ntmp[:, 0, 0:n], hng[:, hh:hh + 1], zogs[:, qsl, t0:t0 + n], ALU.mult,
                    ALU.mult, [B("ntmp", 0), RB("zogs", qsl, tg)] + SM, [B("oaT", hh, tg)])
                yield

        def run_all(*gens, reps=None):
            alive = [(g_, (reps[i] if reps else 1)) for i, g_ in enumerate(gens) if g_ is not None]
            while alive:
                for item in list(alive):
                    g_, r_ = item
                    for _ in range(r_):
                        try:
                            next(g_)
                        except StopIteration:
                            alive.remove(item)
                            break

        pools["st"] = [0, 1]
        pools["g"] = [2, 3, 4]
        pools["o"] = [5, 6, 7]
        pmode[0] = "split"
        def chain(*gens):
            for g_ in gens:
                yield from g_

        run_all(chain(gates(0), transposes(0)))
        for hh in range(4):
            run_all(chain(state_pass(hh), output_pass(hh)),
                    chain(gates(hh + 1), transposes(hh + 1)) if hh < 3 else None, reps=(1, 1))
        pmode[0] = "all"

        new_phase()
        tok_major(WB_POOL, zpool, "zpool")
        for g in range(4):
            sl = g % 2
            pa = pstr[:, sl, 0:8 * 1024].rearrange("p (a b) -> p a b", a=8)
            pbv = pstr[:, sl, 8 * 1024:8 * 1024 + 512].rearrange("p (a b) -> p a b", a=2)
            dma("sp", pa, poolA_d[g].rearrange("(st p) t -> p st t", p=128), [],
                [RB("pstr", sl, 0)] + fz("sp"), ("pstrA", sl))
            dma("sp", pbv, poolB_d[g].rearrange("(st p) t -> p st t", p=128), [], [RB("pstr", sl, 1)], ("pstrB", sl))
            p1 = []
            for tg in range(3):
                t0, n, gg = TGS[tg]
                dma("sp", invb[:, tg:tg + 1, 0:n], invc_d[g:g + 1, t0:t0 + n].partition_broadcast(128), [],
                    [RB("invb", tg)], ("invb", tg))
                pt, pb = ps()
                if tg < 2:
                    for st_ in range(8):
                        mm(pt[:, 0:n], zpool[:, st_, g * 128:(g + 1) * 128], pa[:, st_, t0:t0 + n], st_ == 0, st_ == 7,
                           [RB("zpool", st_), RB("pstr", sl, 0)], [pb])
                else:
                    for st_ in range(2):
                        mm(pt[:, 0:n], zpool[:, 8 + st_, g * 128:(g + 1) * 128], pbv[:, st_, :], st_ == 0, st_ == 1,
                           [RB("zpool", 8 + st_), RB("pstr", sl, 1)], [pb])
                p1.append((pt, pb))
            for tg in range(3):
                t0, n, gg = TGS[tg]
                tt("dve", pmT[:, tg, 0:n], p1[tg][0][:, 0:n], invb[:, tg, 0:n], ALU.mult, [p1[tg][1], RB("invb", tg)],
                   [RB("pmT", tg)])
            p2 = []
            for tg in range(3):
                t0, n, gg = TGS[tg]
                pt2, pb2 = ps()
                mm(pt2[:, 0:n], pwb[:, g, :], pmT[:, tg, 0:n], True, True, [B("pwb"), RB("pmT", tg)], [pb2])
                p2.append((pt2, pb2))
            for tg in range(3):
                t0, n, gg = TGS[tg]
                act(obT[:, g, t0:t0 + n], p2[tg][0][:, 0:n], AF.Copy, [p2[tg][1]] + SM, [B("obT", g, tg)],
                    scale=psc[:, g:g + 1])

        new_phase()
        for n_ in range(8):
            k = WB_ML[n_]
            s = wget(k)
            wga = rslice(s, 0, (KC, 128))
            wgb = rslice(s, KC * 128, (KC, 128))
            wa = rslice(s, 2 * KC * 128, (4, 128))
            wb = rslice(s, 2 * KC * 128 + 512, (4, 128))
            for tg in range(3):
                t0, n, g = TGS[tg]
                pga, pgab = feat_major(lambda kc, w=wga: w[:, kc, :], wbufs(k), tg)
                pgb, pgbb = feat_major(lambda kc, w=wgb: w[:, kc, :], wbufs(k), tg)
                pA, pAb = ps()
                for e in range(4):
                    mm(pA[:, 0:n], wa[:, e, :], oaT[:, e, t0:t0 + n], e == 0, e == 3, wbufs(k) + [B("oaT", e, tg)], [pAb])
                pB, pBb = ps()
                for e in range(4):
                    mm(pB[:, 0:n], wb[:, e, :], obT[:, e, t0:t0 + n], e == 0, e == 3, wbufs(k) + [B("obT", e, tg)], [pBb])
                act(sg32[:, 0, 0:n], pga[:, 0:n], AF.Sigmoid, [pgab], [RB("sg32", 0)] + fz("act"))
                act(sg32[:, 1, 0:n], pgb[:, 0:n], AF.Sigmoid, [pgbb], [RB("sg32", 1)])
                tt("dve", m1[:, 0, 0:n], pA[:, 0:n], sg32[:, 0, 0:n], ALU.mult, [pAb, RB("sg32", 0)],
                   [RB("m1", 0)] + fz("dve"))
                tt("dve", m1[:, 1, 0:n], pB[:, 0:n], sg32[:, 1, 0:n], ALU.mult, [pBb, RB("sg32", 1)], [RB("m1", 1)])
                tt("dve", mergedT[:, n_, t0:t0 + n], m1[:, 0, 0:n], m1[:, 1, 0:n], ALU.add,
                   [RB("m1", 0), RB("m1", 1)], [RB("mergedT", n_, tg)])
                it_ = n_ * 3 + tg
                mini_consume(16 + it_)
                if it_ == 23:
                    make_G(G2, g2c, 32, "2")

        def resid_proj(kbase, nblk, src, sname, nk, gate_c0, kcs_per_blk):
            for b in range(nblk):
                k = kbase + b
                s = wget(k)
                wv = rslice(s, 0, (nk, kcs_per_blk * 128))
                for jj in range(kcs_per_blk):
                    n2 = b * kcs_per_blk + jj
                    for tg in range(3):
                        t0, n, g = TGS[tg]
                        pt, pb = ps()
                        for kc in range(nk):
                            mm(pt[:, 0:n], wv[:, kc, jj * 128:(jj + 1) * 128], src[:, kc, t0:t0 + n], kc == 0,
                               kc == nk - 1, wbufs(k) + [RB(sname, kc, tg)], [pb])
                        stt(xT[:, n2, t0:t0 + n], pt[:, 0:n], modT[:, gate_c0 + n2, g:g + 1], xT[:, n2, t0:t0 + n],
                            ALU.mult, ALU.add, [pb, B("xT", n2, tg), MB(gate_c0), MB(gate_c0 + 4)], [B("xT", n2, tg)])

        resid_proj(WB_O, 2, mergedT, "mergedT", KC, 16, 4)
        norm_to_hT(G2, "2", 24)

        new_phase()
        for b in range(11):
            k = WB_FL[b]
            s = wget(k)
            for jj in range(2):
                j = 2 * b + jj
                wg = rslice(s, (2 * jj) * KC * 128, (KC, 128))
                wu = rslice(s, (2 * jj + 1) * KC * 128, (KC, 128))
                for tg in range(3):
                    t0, n, g = TGS[tg]
                    pg, pgb_ = feat_major(lambda kc, w=wg: w[:, kc, :], wbufs(k), tg)
                    pu, pub_ = feat_major(lambda kc, w=wu: w[:, kc, :], wbufs(k), tg)
                    sl = (j * 3 + tg) % 2
                    act(ntmp[:, sl, 0:n], pg[:, 0:n], AF.Silu, [pgb_], [B("ntmp", sl)])
                    tt("dve", actT[:, j, t0:t0 + n], pu[:, 0:n], ntmp[:, sl, 0:n], ALU.mult,
                       [pub_, B("ntmp", sl)], [RB("actT", j, tg)] + fz("dve"))
                    it_ = j * 3 + tg
                    if it_ < 8:
                        mini_consume(40 + it_)

        resid_proj(WB_FO, 8, actT, "actT", FC, 40, 1)

        yb_first = set()
        sq_eng[0] = "act"
        for tg in range(3):
            t0, n, g = TGS[tg]
            rs = rms_rstd(tg)
            for kc in range(KC):
                stt(xT[:, kc, t0:t0 + n], xT[:, kc, t0:t0 + n], fgc[:, kc:kc + 1], rstd2[:, rs, 0:n], ALU.mult,
                    ALU.mult, [B("xT", kc, tg), B("rstd", rs)] + SM, [B("xT", kc, tg)])
            LFall = [B("LF", a, b) for a in range(2) for b in range(2)]
            for ii in range(n // 128):
                i = t0 // 128 + ii
                s4 = i % 4
                stg, sl = (ytok, s4) if s4 < 2 else (ytokB, s4 - 2)
                sb_ = YB[sl] if s4 < 2 else [B("ytokB", sl, 0), B("ytokB", sl, 1)]
                for hf in range(2):
                    pt, pb = ps()
                    for j in range(4):
                        kc = hf * 4 + j
                        tr(pt[:, j * 128:(j + 1) * 128], xT[:, kc, i * 128:(i + 1) * 128], ident32[:],
                           [B("xT", kc, tg)] + SM, [pb])
                    wr = [sb_[hf]]
                    if s4 >= 2 and (sl, hf) not in yb_first:
                        yb_first.add((sl, hf))
                        wr = wr + LFall
                    cp("act" if hf == 0 else "dve", stg[:, sl, hf * 512:(hf + 1) * 512], pt[:, :], [pb], wr)
                dma("sp", y_d[i * 128:(i + 1) * 128, :], stg[:, sl, :], sb_, [], ("yout", s4))

        if debug:
            dma("sp", dbg_oaT, oaT[:], [B("oaT", a, b) for a in range(4) for b in range(3)], [], ("dbg",))
            dma("sp", dbg_obT, obT[:], [B("obT", a, b) for a in range(4) for b in range(3)], [], ("dbg",))
        fk = [("dbg",), ("yout", 0), ("yout", 1), ("yout", 2), ("yout", 3), ("stout", 0), ("stout", 1)]
        P.emit(nc, stack, [k_ for k_ in fk if k_ in P.dma_counts])
    return nc


_CACHE = {}


def kernel(x_prompt, x_sample, state_hgrn, c, c_ctx, w_ada, b_ada, norm1_g, w_in, hgrn_lb_logits, hgrn_norm_g,
           w_branch_a, pool_w, pool_scale, w_branch_b, w_out, norm2_g, w_ffn_in, w_ffn_out, final_g):
    f32 = lambda a: np.ascontiguousarray(np.asarray(a, dtype=np.float32))
    x_prompt, x_sample, state_hgrn, c, c_ctx = map(f32, (x_prompt, x_sample, state_hgrn, c, c_ctx))
    import os
    dbg = bool(os.environ.get("KDEBUG"))
    if "nc" not in _CACHE:
        _CACHE["nc"] = build_nc(debug=dbg)
        _CACHE["pc"] = _pool_consts()
        _CACHE["sc"] = _scan_consts()
    nc = _CACHE["nc"]
    p1, i1, p2, i2 = _CACHE["pc"]
    mf, mb, rmask = _CACHE["sc"]
    bf = ml_dtypes.bfloat16

    p1T = np.ascontiguousarray(p1.transpose(0, 2, 1))
    p2T = np.ascontiguousarray(p2.transpose(0, 2, 1))
    poolA_prompt = np.zeros((4, 1024, 1024), np.float32)
    for q in range(4):
        poolA_prompt[:, q * 256:(q + 1) * 256, q * 256:(q + 1) * 256] = p1T
    poolA_prompt = poolA_prompt.astype(bf)
    poolA_sample = p2T.astype(bf)
    poolB = p1T.astype(bf)
    inv_prompt = np.concatenate([np.tile(i1, (1, 4)), i1], axis=1)
    inv_sample = np.concatenate([i2, i1], axis=1)

    lbl = f32(hgrn_lb_logits)
    shared = {
        "mf": mf.astype(bf), "mb": mb.astype(bf), "rmask": rmask, "id32": np.eye(128, dtype=np.float32),
        "w_ada": f32(w_ada[0]), "b_adaT": _colT(b_ada[0], 48), "n1gT": _colT(norm1_g[0], 8),
        "n2gT": _colT(norm2_g[0], 8), "fgT": _colT(final_g, 8), "w_in": f32(w_in[0]),
        "lb0T": _colT(lbl[0].reshape(-1), 8), "lb1T": _colT(lbl[1].reshape(-1), 8),
        "hngT": _colT(hgrn_norm_g[0], 4), "wba": f32(w_branch_a[0]), "poolw": f32(pool_w[0]),
        "pscT": _colT(pool_scale[0], 4), "wbb": f32(w_branch_b[0]), "wout": f32(w_out[0]),
        "wffin": f32(w_ffn_in[0]), "wffout": f32(w_ffn_out[0]), "poolB": poolB,
    }
    in_maps = []
    plan = []
    for core in range(N_CORES):
        if core < 2:
            segs = [("s", core, q) for q in range(4)] + [("p", core, 0)]
            x = np.concatenate([x_sample[core], x_prompt[core]], axis=0)
            cv = np.stack([c[core], c_ctx])
            s0 = np.zeros((5, 2, 4, 128, 128), np.float32)
            s0[0, 0] = state_hgrn[core, 0, 0]
            s0[3, 1] = state_hgrn[core, 0, 1]
            flags = np.zeros((128, 10), np.float32)
            flags[:, 1:4] = 1.0
            flags[:, 5:8] = 1.0
            pa, inv = poolA_sample, inv_sample
        else:
            ids = [2 + (core - 2) * 5 + q for q in range(5)]
            segs = [("p", i, 0) for i in ids]
            x = np.concatenate([x_prompt[i] for i in ids], axis=0)
            cv = np.stack([c_ctx, c_ctx])
            s0 = np.zeros((5, 2, 4, 128, 128), np.float32)
            flags = np.zeros((128, 10), np.float32)
            pa, inv = poolA_prompt, inv_prompt
        plan.append(segs)
        cT = np.ascontiguousarray(cv.reshape(2, 8, 128).transpose(2, 0, 1).reshape(128, 16))
        m = dict(shared)
        m.update({"x": np.ascontiguousarray(x), "cT": cT, "s0": s0, "flags": flags, "poolA": pa,
                  "invc": np.ascontiguousarray(inv.astype(np.float32))})
        in_maps.append(m)

    res = run_bass_kernel_spmd(nc, in_maps, core_ids=list(range(N_CORES)))
    if dbg:
        _CACHE["dbg"] = [{k_: np.asarray(v_) for k_, v_ in r.items()} for r in res.results]
    y_prompt = np.zeros((32, 256, D), np.float32)
    y_sample = np.zeros((2, 1024, D), np.float32)
    new_state = np.zeros((32, 1, 2, 4, 128, 128), np.float32)
    for core in range(N_CORES):
        y = np.asarray(res.results[core]["y"], dtype=np.float32)
        st = np.asarray(res.results[core]["st"], dtype=np.float32)
        for sgi, (kind, idx, q) in enumerate(plan[core]):
            ys = y[sgi * 256:(sgi + 1) * 256]
            if kind == "s":
                y_sample[idx, q * 256:(q + 1) * 256] = ys
            else:
                y_prompt[idx] = ys
                new_state[idx, 0] = st[sgi]
    return (y_prompt, y_sample, new_state)
```

```python
import numpy as np
import ml_dtypes
import concourse.bass as bass
import concourse.mybir as mybir
from concourse.bass_utils import run_bass_kernel_spmd

F32 = mybir.dt.float32
BF16 = mybir.dt.bfloat16
AF = mybir.ActivationFunctionType
ALU = mybir.AluOpType

D = 1024
KC = 8
T = 1280
NT = 10
D_A = 512
D_FF = 2816
FC = 22
IN_COLS = 5120
EPS = 1e-6
CH = 64
NCH = 20
TGS = [(0, 512, 0), (512, 512, 0), (1024, 256, 1)]
N_CORES = 8
SLOT = 4096
NSLOT = 2
SYNC_SAME_ENGINE = True


class Buf:
    __slots__ = ("name", "w", "rs")

    def __init__(self, name):
        self.name = name
        self.w = None
        self.rs = []


class Strong:
    __slots__ = ("b",)

    def __init__(self, b):
        self.b = b


class Op:
    __slots__ = ("eng", "fn", "deps", "key", "sig", "tick", "dcount")

    def __init__(self, eng, fn, deps, key):
        self.eng = eng
        self.fn = fn
        self.deps = deps
        self.key = key
        self.sig = False
        self.tick = 0
        self.dcount = 0


class Prog:
    def __init__(self):
        self.ops = []
        self.dma_counts = {}

    def add(self, eng, fn, reads=(), writes=(), key=None):
        idx = len(self.ops)
        deps = {}
        strong = [b.b for b in writes if isinstance(b, Strong)]
        writes = [b.b if isinstance(b, Strong) else b for b in writes]
        for b in strong:
            if b.w is not None:
                deps[b.w] = "raw"
            for r in b.rs:
                deps[r] = "raw"

        def dep(i, kind):
            if i is None:
                return
            if i not in deps or kind == "raw":
                deps[i] = kind

        for b in reads:
            dep(b.w, "raw")
        for b in writes:
            dep(b.w, "waw")
            for r in b.rs:
                dep(r, "war")
        for b in reads:
            b.rs.append(idx)
        for b in writes:
            b.w = idx
            b.rs = []
        op = Op(eng, fn, deps, key)
        if key is not None:
            self.dma_counts[key] = self.dma_counts.get(key, 0) + 1
            op.dcount = self.dma_counts[key]
        self.ops.append(op)
        return idx

    @staticmethod
    def _needs(p, q, kind):
        if p.key is not None:
            return True
        if q.key is not None:
            return True
        if p.eng == q.eng and (p.eng == "pe" or (kind != "raw" and not SYNC_SAME_ENGINE)):
            return False
        return True

    def emit(self, nc, stack, final_keys):
        ops = self.ops
        for q in ops:
            best = {}
            for i, kind in q.deps.items():
                p = ops[i]
                if p.key is None and self._needs(p, q, kind):
                    if best.get(p.eng, -1) < i:
                        best[p.eng] = i
            q.deps = {i: k_ for i, k_ in q.deps.items() if ops[i].key is not None or best.get(ops[i].eng) == i}
            for i in best.values():
                ops[i].sig = True
        cnt = {}
        for p in ops:
            if p.sig:
                cnt[p.eng] = cnt.get(p.eng, 0) + 1
                p.tick = cnt[p.eng]
        eng_sem = {e: stack.enter_context(nc.semaphore("tick_" + e)) for e in ("pe", "act", "dve", "pool")}
        dma_sem = {k: stack.enter_context(nc.semaphore("dma_%s" % (k,))) for k in self.dma_counts}
        block = stack.enter_context(nc.Block())

        def run_engine(name, h):
            seen = {}
            for q in ops:
                if q.eng != name:
                    continue
                waits = {}
                for i, kind in q.deps.items():
                    p = ops[i]
                    if p.key is not None:
                        s, v = dma_sem[p.key], 16 * p.dcount
                    else:
                        if not self._needs(p, q, kind):
                            continue
                        s, v = eng_sem[p.eng], p.tick
                    if waits.get(s, (0,))[0] < v:
                        waits[s] = (v,)
                for s, (v,) in waits.items():
                    if seen.get(s, 0) >= v:
                        continue
                    h.wait_ge(s, v)
                    seen[s] = v
                ins = q.fn(h)
                if q.key is not None:
                    ins.then_inc(dma_sem[q.key], 16)
                elif q.sig:
                    ins.then_inc(eng_sem[q.eng], 1)
            if name == "sp":
                for k in final_keys:
                    h.wait_ge(dma_sem[k], 16 * self.dma_counts[k])

        @block.tensor
        def _(h):
            run_engine("pe", h)

        @block.scalar
        def _(h):
            run_engine("act", h)

        @block.vector
        def _(h):
            run_engine("dve", h)

        @block.gpsimd
        def _(h):
            run_engine("pool", h)

        @block.sync
        def _(h):
            run_engine("sp", h)


def _pool_consts():
    wins = (2, 4, 8, 16)
    p1 = np.zeros((4, 256, 256), np.float32)
    i1 = np.zeros((4, 256), np.float32)
    t = np.arange(256)
    for g, w in enumerate(wins):
        lo = np.clip(t - w // 2, 0, 255)
        hi = np.clip(t + w // 2 - 1, 0, 255)
        for tt in range(256):
            p1[g, tt, lo[tt]:hi[tt] + 1] = 1.0
            cnt = hi[tt] - lo[tt] + 1
            p1[g, tt, tt] -= cnt
            i1[g, tt] = 1.0 / cnt
    p2 = np.zeros((4, 1024, 1024), np.float32)
    i2 = np.zeros((4, 1024), np.float32)
    r = np.arange(16)
    c = np.arange(64)
    for g, w in enumerate(wins):
        rlo, rhi = np.clip(r - w // 2, 0, 15), np.clip(r + w // 2 - 1, 0, 15)
        clo, chi = np.clip(c - w // 2, 0, 63), np.clip(c + w // 2 - 1, 0, 63)
        blk = p2[g].reshape(16, 64, 16, 64)
        for rr in range(16):
            for cc in range(64):
                blk[rr, cc, rlo[rr]:rhi[rr] + 1, clo[cc]:chi[cc] + 1] = 1.0
                cnt = (rhi[rr] - rlo[rr] + 1) * (chi[cc] - clo[cc] + 1)
                blk[rr, cc, rr, cc] -= cnt
                i2[g, rr * 64 + cc] = 1.0 / cnt
    return p1, i1, p2, i2


def _scan_consts():
    s = np.arange(128)[:, None]
    t = np.arange(128)[None, :]
    same = (s // CH) == (t // CH)
    mf = (same & (s <= t)).astype(np.float32)
    mb = (same & (s >= t)).astype(np.float32)
    mf = np.tile(mf, (1, 4))
    mb = np.tile(mb, (1, 4))
    rmask = np.ones((128, 512), np.float32)
    rmask[:, ::CH] = 0.0
    return mf, mb, rmask


def _colT(v, n):
    return np.ascontiguousarray(np.asarray(v, np.float32).reshape(n, 128).T)


def build_nc(debug=False):
    from contextlib import ExitStack

    nc = bass.Bass("TRN2", target_bir_lowering=False)
    P = Prog()

    def din(name, shape, dt=F32):
        return nc.dram_tensor(name, list(shape), dt, kind="ExternalInput").ap()

    x_d = din("x", [T, D])
    cT_d = din("cT", [128, 16])
    s0_d = din("s0", [5, 2, 4, 128, 128])
    flags_d = din("flags", [128, 10])
    poolA_d = din("poolA", [4, 1024, 1024], BF16)
    poolB_d = din("poolB", [4, 256, 256], BF16)
    invc_d = din("invc", [4, T])
    mf_d = din("mf", [128, 512], BF16)
    mb_d = din("mb", [128, 512], BF16)
    rmask_d = din("rmask", [128, 512])
    id32_d = din("id32", [128, 128])
    w_ada_d = din("w_ada", [D, 6 * D])
    b_adaT_d = din("b_adaT", [128, 48])
    n1gT_d = din("n1gT", [128, 8])
    n2gT_d = din("n2gT", [128, 8])
    fgT_d = din("fgT", [128, 8])
    w_in_d = din("w_in", [D, IN_COLS])
    lb0T_d = din("lb0T", [128, 8])
    lb1T_d = din("lb1T", [128, 8])
    hngT_d = din("hngT", [128, 4])
    wba_d = din("wba", [D_A, D])
    poolw_d = din("poolw", [4, 128, 128])
    pscT_d = din("pscT", [128, 4])
    wbb_d = din("wbb", [D_A, D])
    wout_d = din("wout", [D, D])
    n_ffin_d = din("wffin", [D, 2 * D_FF])
    wffout_d = din("wffout", [D_FF, D])
    y_d = nc.dram_tensor("y", [T, D], F32, kind="ExternalOutput").ap()
    st_d = nc.dram_tensor("st", [5, 2, 4, 128, 128], F32, kind="ExternalOutput").ap()
    if debug:
        dbg_hT = nc.dram_tensor("dbg_hT", [128, KC, T], BF16, kind="ExternalOutput").ap()
        dbg_oaT = nc.dram_tensor("dbg_oaT", [128, 4, T], BF16, kind="ExternalOutput").ap()
        dbg_obT = nc.dram_tensor("dbg_obT", [128, 4, T], BF16, kind="ExternalOutput").ap()
        dbg_qk = nc.dram_tensor("dbg_qk", [128, 4, 4, T], BF16, kind="ExternalOutput").ap()
        dbg_v = nc.dram_tensor("dbg_v", [128, NT, 512], BF16, kind="ExternalOutput").ap()

    stack = ExitStack()
    with stack:
        ARENA = 212800
        arena = nc.alloc_sbuf_tensor("arena", [128, ARENA // 4], F32)
        base = nc.lookup_mloc(arena).addr
        off = [0]

        def alloc(name, shape, dt, at=None):
            nbytes = int(np.prod(shape[1:])) * (4 if dt == F32 else 2)
            nbytes = (nbytes + 31) // 32 * 32
            if at is None:
                at = off[0]
                off[0] += nbytes
                assert off[0] <= ARENA, (name, off[0])
            return nc.alloc_sbuf_tensor_at(name, list(shape), dt, offset=base + at), at, nbytes

        xT, _, _ = alloc("xT", [128, KC, T], F32)
        hT, _, _ = alloc("hT", [128, KC, T], BF16)
        ring, _, _ = alloc("ring", [128, NSLOT, SLOT], BF16)
        ident32, _, _ = alloc("ident32", [128, 128], F32)
        identb, _, _ = alloc("identb", [128, 128], BF16)
        onesb, _, _ = alloc("onesb", [128, 128], BF16)
        mfm, _, _ = alloc("mfm", [128, 512], BF16)
        mbm, _, _ = alloc("mbm", [128, 512], BF16)
        rmask, _, _ = alloc("rmask", [128, 512], F32)
        cT, _, _ = alloc("cT", [128, 16], F32)
        scT, _, _ = alloc("scT", [128, 2, 8], BF16)
        badaT, _, _ = alloc("badaT", [128, 48], F32)
        modT, _, _ = alloc("modT", [128, 48, 2], F32)
        g1c, _, _ = alloc("g1c", [128, 8], F32)
        g2c, _, _ = alloc("g2c", [128, 8], F32)
        fgc, _, _ = alloc("fgc", [128, 8], F32)
        G1, _, _ = alloc("G1", [128, 8, 2], F32)
        G2, _, _ = alloc("G2", [128, 8, 2], F32)
        lb0, _, _ = alloc("lb0", [128, 8], F32)
        lb1, _, _ = alloc("lb1", [128, 8], F32)
        lbc, _, _ = alloc("lbc", [128, 8], F32)
        omlb, _, _ = alloc("omlb", [128, 8], F32)
        nomlb, _, _ = alloc("nomlb", [128, 8], F32)
        hng, _, _ = alloc("hng", [128, 4], F32)
        psc, _, _ = alloc("psc", [128, 4], F32)
        flg, _, _ = alloc("flg", [128, 10], F32)
        pw32, _, _ = alloc("pw32", [128, 4, 128], F32)
        pwb, _, _ = alloc("pwb", [128, 4, 128], BF16)
        sqb, _, _ = alloc("sqb", [128, 4, 512], BF16)
        rstd2, _, _ = alloc("rstd", [128, 2, 512], F32)
        ntmp, _, _ = alloc("ntmp", [128, 2, 512], F32)
        oaT, _, _ = alloc("oaT", [128, 4, T], BF16)
        obT, obo, _ = alloc("obT", [128, 4, T], BF16)
        ktokB = nc.alloc_sbuf_tensor_at("ktokB", [128, 2, 2, NT, 128], BF16, offset=base + obo)
        R0 = off[0]
        qk, _, _ = alloc("qk", [128, 2, 4, T], BF16)
        vtok, _, _ = alloc("vtok", [128, NT, 512], BF16)
        zogs, _, _ = alloc("zogs", [128, 2, T], BF16)
        ktok, _, _ = alloc("ktok", [128, 2, 2, NT, 128], BF16)
        sbf, _, _ = alloc("sbf", [128, 2, NCH, 128], BF16)
        assert off[0] - R0 == 56320, off[0] - R0
        csc, _, _ = alloc("csc", [128, 2, 2, NCH], F32)
        ctmp, _, _ = alloc("ctmp", [128, 8], F32)
        Xb, _, _ = alloc("Xb", [128, 2, 2, 128], F32)
        carry, _, _ = alloc("carry", [128, 2, 128], F32)
        s0t, _, _ = alloc("s0t", [128, 2, 2, 128], F32)
        sct, _, _ = alloc("sct", [128, 2, 512], BF16)
        off[0] = max(off[0], R0 + FC * T * 2)
        q32, q32o, _ = alloc("q32", [128, 2, 512], F32)
        minir = nc.alloc_sbuf_tensor_at("minir", [128, 2, KC * 128], BF16, offset=base + q32o)
        LFt, lfo, _ = alloc("LFt", [128, 2, 2, 520], F32)
        KKt, _, _ = alloc("KKt", [128, 2, 2, 512], BF16)
        SGt, g4o, _ = alloc("SGt", [128, 2, 2, 512], F32)
        BBt, _, _ = alloc("BBt", [128, 2, 512], F32)
        ytok = nc.alloc_sbuf_tensor_at("ytok", [128, 2, D], F32, offset=base + g4o)
        ytokB = nc.alloc_sbuf_tensor_at("ytokB", [128, 2, D], F32, offset=base + lfo)
        assert off[0] <= ARENA, off[0]
        PSL = 8 * 1024 + 2 * 256
        pstr = nc.alloc_sbuf_tensor_at("pstr", [128, 2, PSL], BF16, offset=base + R0)
        zpool = nc.alloc_sbuf_tensor_at("zpool", [128, NT, 512], BF16, offset=base + R0 + 34816)
        invb = nc.alloc_sbuf_tensor_at("invb", [128, 3, 512], F32, offset=base + R0 + 45056)
        pmT = nc.alloc_sbuf_tensor_at("pmT", [128, 3, 512], BF16, offset=base + R0 + 51200)
        assert 51200 + 3072 <= 56320
        mergedT = nc.alloc_sbuf_tensor_at("mergedT", [128, KC, T], BF16, offset=base + R0)
        sg32 = nc.alloc_sbuf_tensor_at("sg32", [128, 2, 512], F32, offset=base + R0 + 20480)
        m1 = nc.alloc_sbuf_tensor_at("m1", [128, 2, 512], F32, offset=base + R0 + 24576)
        actT = nc.alloc_sbuf_tensor_at("actT", [128, FC, T], BF16, offset=base + R0)

        banks = [nc.alloc_psum_tensor("ps%d" % i, [128, 512], F32) for i in range(8)]
        bbuf = [Buf("ps%d" % i) for i in range(8)]
        bctr = {}
        pools = {"all": list(range(8))}
        pmode = ["all"]

        def ps(pool=None):
            name = pool if (pool is not None and pmode[0] == "split") else "all"
            lst = pools[name]
            c = bctr.get(name, 0)
            bctr[name] = c + 1
            i = lst[c % len(lst)]
            return banks[i], bbuf[i]

        bufs = {}
        REG = []

        def B(*k):
            if k not in bufs:
                bufs[k] = Buf(str(k))
            return bufs[k]

        def RB(*k):
            if k not in bufs:
                bufs[k] = Buf(str(k))
                REG.append(bufs[k])
            return bufs[k]

        fence = {"act": [], "dve": [], "sp": []}

        def new_phase():
            for e in fence:
                fence[e] = list(REG)

        def fz(eng):
            r = [Strong(b_) for b_ in fence[eng]]
            fence[eng] = []
            return r

        def dma(eng, out, in_, reads, writes, key):
            P.add(eng, lambda h, o=out, i=in_: h.dma_start(out=o, in_=i), reads, writes, key)

        def act(out, in_, func, reads, writes, scale=1.0, bias=0.0):
            P.add("act", lambda h, o=out, i=in_, f=func, s=scale, b=bias: h.activation(o, i, f, bias=b, scale=s),
                  reads, writes)

        def tt(eng, out, a, b, op, reads, writes):
            P.add(eng, lambda h, o=out, a=a, b=b, op=op: h.tensor_tensor(o, a, b, op), reads, writes)

        def ts(eng, out, a, s1, s2, op0, op1, reads, writes):
            P.add(eng, lambda h, o=out, a=a, s1=s1, s2=s2, op0=op0, op1=op1: h.tensor_scalar(o, a, s1, s2, op0, op1),
                  reads, writes)

        def stt(out, a, s, b, op0, op1, reads, writes):
            P.add("dve", lambda h, o=out, a=a, s=s, b=b, op0=op0, op1=op1: h.scalar_tensor_tensor(o, a, s, b, op0, op1),
                  reads, writes)

        def cp(eng, out, in_, reads, writes):
            if eng == "act":
                P.add("act", lambda h, o=out, i=in_: h.copy(o, i), reads, writes)
            else:
                P.add(eng, lambda h, o=out, i=in_: h.tensor_copy(o, i), reads, writes)

        def mm(out, lhsT, rhs, start, stop, reads, writes):
            P.add("pe", lambda h, o=out, l=lhsT, r=rhs, s=start, e=stop: h.matmul(o, l, r, start=s, stop=e),
                  reads, writes)

        def tr(out, in_, ident, reads, writes):
            P.add("pe", lambda h, o=out, i=in_, d=ident: h.transpose(o, i, d), reads, writes)

        wblocks = []

        def wview(w_d, c0, n):
            return w_d.rearrange("(kc p) n -> p kc n", p=128)[:, :, c0:c0 + n]

        wstate = {"issued": 0}

        def slot_bufs(s):
            return [B("ring", s, pi) for pi in range(4)]

        def w_issue_upto(k):
            while wstate["issued"] <= min(k, len(wblocks) - 1):
                j = wstate["issued"]
                s = j % NSLOT
                pieces = wblocks[j]
                for pi, (o, shp, src) in enumerate(pieces):
                    n_el = int(np.prod(shp))
                    dst = ring[:, s, o:o + n_el].rearrange("p (a b) -> p a b", a=shp[0])
                    wr = slot_bufs(s) if len(pieces) == 1 else [B("ring", s, pi)]
                    dma("pool", dst, src, [], wr, ("ring", s, pi))
                wstate["issued"] += 1

        def wget(k):
            w_issue_upto(k + NSLOT - 1)
            return k % NSLOT

        def wbufs(k):
            return slot_bufs(k % NSLOT)

        WB_A = {}

        def ada_blk(b):
            WB_A[b] = len(wblocks)
            wblocks.append([(0, (KC, 512), wview(w_ada_d, b * 512, 512))])

        for b in range(4):
            ada_blk(b)
        WB_V = len(wblocks)
        wblocks.append([(0, (KC, 512), wview(w_in_d, 1536, 512))])
        WB_HL = []
        for h_ in range(4):
            WB_HL.append(len(wblocks))
            wblocks.append([(j * KC * 128, (KC, 128), wview(w_in_d, c_ + h_ * 128, 128))
                            for j, c_ in enumerate((0, 512, 1024, 2048))])
        WB_POOL = len(wblocks)
        wblocks.append([(0, (KC, 512), wview(w_in_d, 2560, 512))])
        WB_ML = []
        for n_ in range(8):
            WB_ML.append(len(wblocks))
            wblocks.append([
                (0, (KC, 128), wview(w_in_d, 3072 + n_ * 128, 128)),
                (KC * 128, (KC, 128), wview(w_in_d, 4096 + n_ * 128, 128)),
                (2 * KC * 128, (4, 128), wview(wba_d, n_ * 128, 128)),
                (2 * KC * 128 + 512, (4, 128), wview(wbb_d, n_ * 128, 128)),
            ])
        WB_O = len(wblocks)
        for b in range(2):
            wblocks.append([(0, (KC, 512), wview(wout_d, b * 512, 512))])
        WB_FL = []
        for b in range(11):
            WB_FL.append(len(wblocks))
            wblocks.append([
                (0, (KC, 128), wview(n_ffin_d, (2 * b) * 128, 128)),
                (KC * 128, (KC, 128), wview(n_ffin_d, D_FF + (2 * b) * 128, 128)),
                (2 * KC * 128, (KC, 128), wview(n_ffin_d, (2 * b + 1) * 128, 128)),
                (3 * KC * 128, (KC, 128), wview(n_ffin_d, D_FF + (2 * b + 1) * 128, 128)),
            ])
        WB_FO = len(wblocks)
        for b in range(8):
            wblocks.append([(0, (FC, 128), wview(wffout_d, b * 128, 128))])

        def rslice(s, o, shp):
            n_el = int(np.prod(shp))
            return ring[:, s, o:o + n_el].rearrange("p (a b) -> p a b", a=shp[0])

        def tg_of_tile(i):
            return 0 if i < 4 else (1 if i < 8 else 2)

        YB = [[B("SG", 0, 0), B("SG", 0, 1)], [B("SG", 1, 0), B("SG", 1, 1)]]
        small = [(ident32, id32_d), (cT, cT_d), (badaT, b_adaT_d), (g1c, n1gT_d), (mfm, mf_d), (mbm, mb_d),
                 (rmask, rmask_d), (g2c, n2gT_d), (fgc, fgT_d), (lb0, lb0T_d), (lb1, lb1T_d), (hng, hngT_d),
                 (psc, pscT_d), (flg, flags_d)]
        SM = [B("small", tns.name) for tns, _ in small] + [B("small", "pw32")]

        def x_load(i):
            sl = i % 2
            dma("sp", ytok[:, sl, :], x_d[i * 128:(i + 1) * 128, :], [], YB[sl], ("ytok", sl))

        dma("sp", ident32[:], id32_d, [], [B("small", "ident32")], ("small",))
        x_load(0)
        x_load(1)
        for tns, src_ in small[1:]:
            dma("sp", tns[:], src_, [], [B("small", tns.name)], ("small",))
        dma("sp", pw32[:], poolw_d.rearrange("g c d -> c g d"), [], [B("small", "pw32")], ("small",))

        P.add("pool", lambda h: h.memset(ktok[:].rearrange("p a b c d -> p (a b c d)"), 0.0), [], [RB("ktokz", 0)])
        P.add("pool", lambda h: h.memset(ktokB[:].rearrange("p a b c d -> p (a b c d)"), 0.0), [],
              [RB("ktokz", 1)])
        P.add("dve", lambda h: h.tensor_copy(identb[:], ident32[:]), SM, [B("identb")])
        P.add("dve", lambda h: h.memset(onesb[:], 1.0), [], [B("onesb")])
        P.add("dve", lambda h: h.tensor_copy(pwb[:], pw32[:]), SM, [B("pwb")])
        P.add("dve", lambda h: h.memset(LFt[:, :, :, 0:8], 0.0), [],
              [B("LF", a, b) for a in range(2) for b in range(2)])
        tt("dve", lb1[:], lb0[:], lb1[:], ALU.subtract, SM, [B("lbd")])
        act(lbc[:], lb1[:], AF.Sigmoid, [B("lbd")], [B("lbc")])
        ts("dve", omlb[:], lbc[:], -1.0, 1.0, ALU.mult, ALU.add, [B("lbc")], [B("omlb")])
        ts("dve", nomlb[:], lbc[:], 1.0, -1.0, ALU.mult, ALU.add, [B("lbc")], [B("omlb")])
        act(scT[:].rearrange("p g k -> p (g k)"), cT[:], AF.Silu, SM, [B("scT")])

        def x_tile(i):
            sl = i % 2
            for hf in range(2):
                pt, pb = ps()
                for j in range(4):
                    kc = hf * 4 + j
                    tr(pt[:, j * 128:(j + 1) * 128], ytok[:, sl, kc * 128:(kc + 1) * 128], ident32[:],
                       YB[sl] + SM, [pb])
                cp("dve" if hf == 0 else "act", xT[:, hf * 4:hf * 4 + 4, i * 128:(i + 1) * 128],
                   pt[:, :].rearrange("p (a b) -> p a b", a=4), [pb],
                   [B("xT", hf * 4 + j, tg_of_tile(i)) for j in range(4)])
            if i + 2 < NT:
                x_load(i + 2)

        def MB(c):
            return B("modblk", c // 4)

        def adaln_block(bb):
            k = WB_A[bb]
            s = wget(k)
            wv = rslice(s, 0, (KC, 512))
            pt, pb = ps()
            for j in range(4):
                for kc in range(KC):
                    mm(pt[:, 2 * j:2 * j + 2], wv[:, kc, j * 128:(j + 1) * 128], scT[:, :, kc],
                       kc == 0, kc == KC - 1, wbufs(k) + [B("scT")], [pb])
            tt("dve", modT[:, bb * 4:bb * 4 + 4, :], pt[:, 0:8].rearrange("p (c g) -> p c g", g=2),
               badaT[:, bb * 4:bb * 4 + 4].unsqueeze(2).broadcast_to([128, 4, 2]), ALU.add,
               [pb] + SM, [B("modblk", bb)])

        mini = {"next": 16}

        def mini_issue_upto(c):
            while mini["next"] <= min(c, 47):
                cc = mini["next"]
                sl_ = cc % 2
                dst = minir[:, sl_, :].rearrange("p (a b) -> p a b", a=KC)
                dma("pool", dst, wview(w_ada_d, cc * 128, 128), [], [B("q32", sl_)], ("mini", sl_))
                mini["next"] += 1

        def mini_consume(c):
            mini_issue_upto(c + 1)
            sl_ = c % 2
            pt, pb = ps()
            for kc in range(KC):
                mm(pt[:, 0:2], minir[:, sl_, kc * 128:(kc + 1) * 128], scT[:, :, kc], kc == 0, kc == KC - 1,
                   [B("q32", sl_), B("scT")], [pb])
            ts("dve", modT[:, c, :], pt[:, 0:2], badaT[:, c:c + 1], None, ALU.add, ALU.bypass, [pb] + SM, [MB(c)])

        def make_G(Gt, gcol, sc0, name):
            for g in range(2):
                stt(Gt[:, :, g], modT[:, sc0:sc0 + 8, g], 1.0, gcol[:], ALU.add, ALU.mult,
                    [MB(sc0), MB(sc0 + 4)] + SM, [B("G", name)])

        sq_eng = ["act"]

        def rms_rstd(tg, slot=None):
            t0, n, g = TGS[tg]
            pt, pb = ps()
            for hf in range(2):
                if sq_eng[0] == "act":
                    act(sqb[:, :, 0:n], xT[:, hf * 4:hf * 4 + 4, t0:t0 + n], AF.Square,
                        [B("xT", hf * 4 + j, tg) for j in range(4)], [B("sqb")])
                else:
                    tt("pool", sqb[:, :, 0:n], xT[:, hf * 4:hf * 4 + 4, t0:t0 + n],
                       xT[:, hf * 4:hf * 4 + 4, t0:t0 + n], ALU.mult,
                       [B("xT", hf * 4 + j, tg) for j in range(4)], [B("sqb")])
                for j in range(4):
                    mm(pt[:, 0:n], onesb[:], sqb[:, j, 0:n], hf == 0 and j == 0, hf == 1 and j == 3,
                       [B("sqb"), B("onesb")], [pb])
            rs = tg % 2 if slot is None else slot
            act(rstd2[:, rs, 0:n], pt[:, 0:n], AF.Ln, [pb], [B("rstd", rs)], scale=1.0 / D, bias=EPS)
            act(rstd2[:, rs, 0:n], rstd2[:, rs, 0:n], AF.Exp, [B("rstd", rs)], [B("rstd", rs)], scale=-0.5)
            return rs

        def modulate(Gt, name, shift_c0, tg, rs):
            t0, n, g = TGS[tg]
            for kc in range(KC):
                sl = kc % 2
                stt(ntmp[:, sl, 0:n], xT[:, kc, t0:t0 + n], Gt[:, kc, g:g + 1], rstd2[:, rs, 0:n], ALU.mult, ALU.mult,
                    [B("xT", kc, tg), B("rstd", rs), B("G", name)], [B("ntmp", sl)])
                act(hT[:, kc, t0:t0 + n], ntmp[:, sl, 0:n], AF.Identity,
                    [B("ntmp", sl), MB(shift_c0), MB(shift_c0 + 4)], [B("hT", kc, tg)],
                    bias=modT[:, shift_c0 + kc, g:g + 1])

        def norm_tg(Gt, name, shift_c0, tg):
            rs = rms_rstd(tg)
            modulate(Gt, name, shift_c0, tg, rs)

        def norm_to_hT(Gt, name, shift_c0):
            for tg in range(3):
                norm_tg(Gt, name, shift_c0, tg)

        x_tile(0)
        x_tile(1)
        adaln_block(0)
        x_tile(2)
        x_tile(3)
        r0 = rms_rstd(0, 0)
        adaln_block(1)
        x_tile(4)
        x_tile(5)
        adaln_block(2)
        x_tile(6)
        x_tile(7)
        r1 = rms_rstd(1, 1)
        adaln_block(3)
        make_G(G1, g1c, 8, "1")
        modulate(G1, "1", 0, 0, r0)
        x_tile(8)
        x_tile(9)
        r2 = rms_rstd(2, 0)
        modulate(G1, "1", 0, 1, r1)
        modulate(G1, "1", 0, 2, r2)
        sq_eng[0] = "pool"
        if debug:
            dma("sp", dbg_hT, hT[:], [B("hT", kc, tg) for kc in range(KC) for tg in range(3)], [], ("dbg",))

        def hT_bufs(tg):
            return [B("hT", kc, tg) for kc in range(KC)]

        def tok_major(k, dst, dname, tiles=None):
            s = wget(k)
            wv = rslice(s, 0, (KC, 512))
            for i in (range(NT) if tiles is None else tiles):
                pt, pb = ps()
                for kc in range(KC):
                    mm(pt[:, :], hT[:, kc, i * 128:(i + 1) * 128], wv[:, kc, :], kc == 0, kc == KC - 1,
                       wbufs(k) + hT_bufs(tg_of_tile(i)), [pb])
                eng = "dve" if i % 2 == 0 else "act"
                cp(eng, dst[:, i, :], pt[:, :], [pb], [RB(dname, i)] + fz(eng))

        def feat_major(wv_fn, rbufs, tg, pool=None):
            t0, n, g = TGS[tg]
            pt, pb = ps(pool)
            for kc in range(KC):
                mm(pt[:, 0:n], wv_fn(kc), hT[:, kc, t0:t0 + n], kc == 0, kc == KC - 1, rbufs + hT_bufs(tg), [pb])
            return pt, pb

        new_phase()
        tok_major(WB_V, vtok, "vtok")

        qctr = [0]

        def gates(hh):
            k = WB_HL[hh]
            s = wget(k)
            qsl = hh % 2
            CS = RB("csc", qsl)
            wo = rslice(s, 3 * KC * 128, (KC, 128))
            zp = [feat_major(lambda kc, w=wo: w[:, kc, :], wbufs(k), tg, "g") for tg in range(3)]
            for tg in range(3):
                t0, n, g = TGS[tg]
                act(zogs[:, qsl, t0:t0 + n], zp[tg][0][:, 0:n], AF.Silu, [zp[tg][1]],
                    [RB("zogs", qsl, tg)] + fz("act"))
            yield
            wq = rslice(s, 0, (KC, 128))
            wfs = [rslice(s, (1 + dr) * KC * 128, (KC, 128)) for dr in range(2)]

            def proj(tg, which):
                w = wq if which == 0 else wfs[which - 1]
                return feat_major(lambda kc, w=w: w[:, kc, :], wbufs(k), tg, "g")

            nxt = [proj(0, 0), proj(0, 1), proj(0, 2)]
            yield
            for tg in range(3):
                t0, n, g = TGS[tg]
                nch = n // CH
                c0 = t0 // CH
                qs = qctr[0] % 2
                qctr[0] += 1
                curp = nxt
                nxt = []
                cp("act", q32[:, qs, 0:n], curp[0][0][:, 0:n], [curp[0][1]], [B("q32", qs)])
                pf = [curp[1], curp[2]]
                pr = qs
                SGv = [SGt[:, pr, dr, 0:n] for dr in range(2)]
                LFv = [LFt[:, pr, dr, 8:8 + n] for dr in range(2)]
                LFs = [LFt[:, pr, dr, 7:7 + n] for dr in range(2)]
                BBv = [BBt[:, dr, 0:n] for dr in range(2)]
                KKv = [KKt[:, pr, dr, 0:n] for dr in range(2)]
                bSG = [B("SG", pr, dr) for dr in range(2)]
                bLF = [B("LF", pr, dr) for dr in range(2)]
                bKK = [B("KK", pr, dr) for dr in range(2)]
                li = [dr * 4 + hh for dr in range(2)]
                for dr in range(2):
                    act(SGv[dr], pf[dr][0][:, 0:n], AF.Sigmoid, [pf[dr][1]], [bSG[dr]])
                yield
                if tg < 2:
                    nxt.append(proj(tg + 1, 0))
                for dr in range(2):
                    act(LFv[dr], SGv[dr], AF.Ln, [bSG[dr], B("omlb"), B("lbc")], [bLF[dr]],
                        scale=omlb[:, li[dr]:li[dr] + 1], bias=lbc[:, li[dr]:li[dr] + 1])
                    ts("pool", KKv[dr], SGv[dr], nomlb[:, li[dr]:li[dr] + 1], omlb[:, li[dr]:li[dr] + 1], ALU.mult,
                       ALU.add, [bSG[dr], B("omlb")], [bKK[dr]])
                yield
                if tg < 2:
                    nxt.append(proj(tg + 1, 1))
                P.add("dve", lambda h, n=n, o=BBv[0], l=LFv[0]: h.tensor_tensor_scan(o, rmask[:, 0:n], l, 0.0,
                                                                                     ALU.mult, ALU.add),
                      [bLF[0]] + SM, [B("BB", 0)])
                P.add("dve", lambda h, n=n, o=BBv[1], l=LFs[1]: h.tensor_tensor_scan(o, l, rmask[:, 0:n], 0.0,
                                                                                     ALU.add, ALU.mult),
                      [bLF[1]] + SM, [B("BB", 1)])
                yield
                if tg < 2:
                    nxt.append(proj(tg + 1, 2))
                b3 = [BBv[dr].rearrange("p (c t) -> p c t", t=CH) for dr in range(2)]
                l3 = LFv[1].rearrange("p (c t) -> p c t", t=CH)
                act(SGv[0], BBv[0], AF.Exp, [B("BB", 0)], [bSG[0]], scale=1.0)
                act(LFv[0], BBv[0], AF.Exp, [B("BB", 0)], [bLF[0]], scale=-1.0)
                act(csc[:, qsl, 0, c0:c0 + nch], b3[0][:, :, 63], AF.Exp, [B("BB", 0)], [CS])
                tt("dve", ctmp[:, 0:nch], b3[1][:, :, 63], l3[:, :, 63], ALU.add, [B("BB", 1), bLF[1]],
                   [RB("ctmp")])
                act(SGv[1], BBv[1], AF.Exp, [B("BB", 1)], [bSG[1]], scale=-1.0)
                act(csc[:, qsl, 1, c0:c0 + nch], ctmp[:, 0:nch], AF.Exp, [RB("ctmp")], [CS])
                yield
                tt("dve", qk[:, qsl, 0, t0:t0 + n], q32[:, qs, 0:n], SGv[0], ALU.mult,
                   [B("q32", qs), bSG[0]], [RB("qk", qsl, 0, tg)] + fz("dve"))
                tt("dve", qk[:, qsl, 1, t0:t0 + n], KKv[0], LFv[0], ALU.mult,
                   [bKK[0], bLF[0]], [RB("qk", qsl, 1, tg)])
                act(LFv[1], BBv[1], AF.Exp, [B("BB", 1)], [bLF[1]], scale=1.0)
                tt("dve", qk[:, qsl, 2, t0:t0 + n], q32[:, qs, 0:n], SGv[1], ALU.mult,
                   [B("q32", qs), bSG[1]], [RB("qk", qsl, 2, tg)])
                yield
                tt("dve", qk[:, qsl, 3, t0:t0 + n], KKv[1], LFv[1], ALU.mult,
                   [bKK[1], bLF[1]], [RB("qk", qsl, 3, tg)])
                yield

        def transposes(hh):
            qsl = hh % 2
            kt = ktok if qsl == 0 else ktokB
            if debug:
                dma("sp", dbg_qk[:, hh], qk[:, qsl], [RB("qk", qsl, a, b) for a in range(4) for b in range(3)], [],
                    ("dbg",))
                if hh == 0:
                    dma("sp", dbg_v, vtok[:], [RB("vtok", i) for i in range(NT)], [], ("dbg",))
            for dr in range(2):
                for i0 in (0, 8):
                    nt_ = min(8, NT - i0)
                    pt, pb = ps("o")
                    ptb = pt.bitcast(BF16)
                    for i in range(i0, i0 + nt_):
                        tr(ptb[:, (i - i0) * 128:(i - i0 + 1) * 128], qk[:, qsl, 2 * dr + 1, i * 128:(i + 1) * 128],
                           identb[:], [RB("qk", qsl, 2 * dr + 1, tg_of_tile(i)), B("identb")], [pb])
                    eng = "act" if dr == 0 else "dve"
                    for hf in range(2):
                        rs = slice(hf * 64, (hf + 1) * 64)
                        cp(eng, kt[rs, hf, dr, i0:i0 + nt_, :],
                           ptb[rs, 0:nt_ * 128].rearrange("p (a b) -> p a b", a=nt_), [pb, RB("ktokz", qsl)],
                           [RB("ktok", qsl, dr, i0)])
                    yield

        S_init_done = [False, False]
        cur = [0, 0]
        s0cnt = [0, 0]
        s0slot = {}

        def s0_issue(hh, dr, seg):
            if hh > 3 or (hh, dr, seg) in s0slot:
                return
            ssl = s0cnt[dr] % 2
            s0cnt[dr] += 1
            s0slot[(hh, dr, seg)] = ssl
            dma("sp", s0t[:, dr, ssl, :], s0_d[seg, dr, hh], [], [RB("s0t", dr, ssl)], ("s0t", dr, ssl))

        def s0_next(hh, dr, seg):
            if dr == 0:
                return (hh, 0, seg + 1) if seg < 4 else (hh + 1, 0, 0)
            return (hh, 1, seg - 1) if seg > 0 else (hh + 1, 1, 4)

        def state_pass(hh):
            qsl = hh % 2
            kt = ktok if qsl == 0 else ktokB
            CS = RB("csc", qsl)
            for step in range(NCH):
                for dr in range(2):
                    j = step if dr == 0 else NCH - 1 - step
                    seg = j // 4
                    first = (j % 4 == 0) if dr == 0 else (j % 4 == 3)
                    last = (j % 4 == 3) if dr == 0 else (j % 4 == 0)
                    i, hf = j // 2, j % 2
                    xb = RB("X", dr)
                    a_, b_ = cur[dr], 1 - cur[dr]
                    if first:
                        s0_issue(hh, dr, seg)
                        ssl = s0slot[(hh, dr, seg)]
                        if not S_init_done[dr]:
                            P.add("dve", lambda h, dr=dr: h.memset(carry[:, dr, :], 0.0), [], [RB("carry", dr)])
                            S_init_done[dr] = True
                        fcol = (seg if dr == 0 else 5 + seg)
                        stt(Xb[:, dr, a_, :], carry[:, dr, :], flg[:, fcol:fcol + 1], s0t[:, dr, ssl, :],
                            ALU.mult, ALU.add, [RB("carry", dr), RB("s0t", dr, ssl)] + SM, [xb])
                        s0_issue(*s0_next(hh, dr, seg))
                    pt, pb = ps("st")
                    mm(pt[:, 0:128], kt[:, hf, dr, i, :], vtok[:, i, hh * 128:(hh + 1) * 128], True, True,
                       [RB("ktok", qsl, dr, 0 if i < 8 else 8), RB("vtok", i), RB("ktokz", qsl)], [pb])
                    if dr == 0 and first:
                        cj = 1.0
                    elif dr == 0:
                        cj = csc[:, qsl, 0, j - 1:j]
                    else:
                        cj = csc[:, qsl, 1, j:j + 1]
                    ts("pool", sbf[:, dr, j, :], Xb[:, dr, a_, :], cj, 0.0, ALU.mult, ALU.add, [xb, CS],
                       [RB("sbf", dr, j // 8)])
                    stt(Xb[:, dr, b_, :], Xb[:, dr, a_, :], cj, pt[:, 0:128], ALU.mult, ALU.add, [xb, pb, CS], [xb])
                    cur[dr] = b_
                    if last:
                        if dr == 0:
                            ts("dve", carry[:, dr, :], Xb[:, dr, b_, :], csc[:, qsl, 0, j:j + 1], None, ALU.mult,
                               ALU.bypass, [xb, CS], [RB("carry", dr)])
                        else:
                            cp("dve", carry[:, dr, :], Xb[:, dr, b_, :], [xb], [RB("carry", dr)])
                        dma("sp", st_d[seg, dr, hh], carry[:, dr, :], [RB("carry", dr)], [], ("stout", dr))
                yield

        def output_pass(hh):
            qsl = hh % 2
            for tg in range(3):
                t0, n, g = TGS[tg]
                ntile = n // 128
                i0 = t0 // 128
                for dr in range(2):
                    pt, pb = ps("o")
                    for ii in range(ntile):
                        i = i0 + ii
                        mm(pt[:, ii * 128:(ii + 1) * 128], qk[:, qsl, 2 * dr + 1, i * 128:(i + 1) * 128],
                           qk[:, qsl, 2 * dr, i * 128:(i + 1) * 128], True, True,
                           [RB("qk", qsl, 2 * dr + 1, tg), RB("qk", qsl, 2 * dr, tg)], [pb])
                    tt("dve", sct[:, dr, 0:n], pt[:, 0:n], (mfm if dr == 0 else mbm)[:, 0:n], ALU.mult,
                       [pb] + SM, [RB("sct", dr)])
                po, pob = ps("o")
                for ii in range(ntile):
                    i = i0 + ii
                    cs = slice(ii * 128, (ii + 1) * 128)
                    vl = vtok[:, i, hh * 128:(hh + 1) * 128]
                    mm(po[:, cs], vl, sct[:, 0, cs], True, False, [RB("vtok", i), RB("sct", 0)], [pob])
                    mm(po[:, cs], vl, sct[:, 1, cs], False, False, [RB("vtok", i), RB("sct", 1)], [pob])
                    for dr in range(2):
                        for hf in range(2):
                            j = i * 2 + hf
                            c2 = slice(ii * 128 + hf * 64, ii * 128 + (hf + 1) * 64)
                            mm(po[:, c2], sbf[:, dr, j, :], qk[:, qsl, 2 * dr, j * 64:(j + 1) * 64], False,
                               (dr == 1 and hf == 1), [RB("sbf", dr, j // 8), RB("qk", qsl, 2 * dr, tg)], [pob])
                act(sqb[:, 0, 0:n], po[:, 0:n], AF.Square, [pob], [B("sqb")])
                pt, pb = ps("o")
                mm(pt[:, 0:n], onesb[:], sqb[:, 0, 0:n], True, True, [B("sqb"), B("onesb")], [pb])
                act(rstd2[:, 0, 0:n], pt[:, 0:n], AF.Ln, [pb], [B("rstd", 0)], scale=1.0 / 128, bias=EPS)
                act(rstd2[:, 0, 0:n], rstd2[:, 0, 0:n], AF.Exp, [B("rstd", 0)], [B("rstd", 0)], scale=-0.5)
                tt("dve", ntmp[:, 0, 0:n], po[:, 0:n], rstd2[:, 0, 0:n], ALU.mult, [pob, B("rstd", 0)],
                   [B("ntmp", 0)])
                stt(oaT[:, hh, t0:t0 + n], ntmp[:, 0, 0:n], hng[:, hh:hh + 1], zogs[:, qsl, t0:t0 + n], ALU.mult,
                    ALU.mult, [B("ntmp", 0), RB("zogs", qsl, tg)] + SM, [B("oaT", hh, tg)])
                yield

        def run_all(*gens, reps=None):
            alive = [(g_, (reps[i] if reps else 1)) for i, g_ in enumerate(gens) if g_ is not None]
            while alive:
                for item in list(alive):
                    g_, r_ = item
                    for _ in range(r_):
                        try:
                            next(g_)
                        except StopIteration:
                            alive.remove(item)
                            break

        pools["st"] = [0, 1]
        pools["g"] = [2, 3, 4]
        pools["o"] = [5, 6, 7]
        pmode[0] = "split"
        def chain(*gens):
            for g_ in gens:
                yield from g_

        run_all(chain(gates(0), transposes(0)))
        for hh in range(4):
            run_all(chain(state_pass(hh), output_pass(hh)),
                    chain(gates(hh + 1), transposes(hh + 1)) if hh < 3 else None, reps=(1, 1))
        pmode[0] = "all"

        new_phase()
        tok_major(WB_POOL, zpool, "zpool")
        for g in range(4):
            sl = g % 2
            pa = pstr[:, sl, 0:8 * 1024].rearrange("p (a b) -> p a b", a=8)
            pbv = pstr[:, sl, 8 * 1024:8 * 1024 + 512].rearrange("p (a b) -> p a b", a=2)
            dma("sp", pa, poolA_d[g].rearrange("(st p) t -> p st t", p=128), [],
                [RB("pstr", sl, 0)] + fz("sp"), ("pstrA", sl))
            dma("sp", pbv, poolB_d[g].rearrange("(st p) t -> p st t", p=128), [], [RB("pstr", sl, 1)], ("pstrB", sl))
            p1 = []
            for tg in range(3):
                t0, n, gg = TGS[tg]
                dma("sp", invb[:, tg:tg + 1, 0:n], invc_d[g:g + 1, t0:t0 + n].partition_broadcast(128), [],
                    [RB("invb", tg)], ("invb", tg))
                pt, pb = ps()
                if tg < 2:
                    for st_ in range(8):
                        mm(pt[:, 0:n], zpool[:, st_, g * 128:(g + 1) * 128], pa[:, st_, t0:t0 + n], st_ == 0, st_ == 7,
                           [RB("zpool", st_), RB("pstr", sl, 0)], [pb])
                else:
                    for st_ in range(2):
                        mm(pt[:, 0:n], zpool[:, 8 + st_, g * 128:(g + 1) * 128], pbv[:, st_, :], st_ == 0, st_ == 1,
                           [RB("zpool", 8 + st_), RB("pstr", sl, 1)], [pb])
                p1.append((pt, pb))
            for tg in range(3):
                t0, n, gg = TGS[tg]
                tt("dve", pmT[:, tg, 0:n], p1[tg][0][:, 0:n], invb[:, tg, 0:n], ALU.mult, [p1[tg][1], RB("invb", tg)],
                   [RB("pmT", tg)])
            p2 = []
            for tg in range(3):
                t0, n, gg = TGS[tg]
                pt2, pb2 = ps()
                mm(pt2[:, 0:n], pwb[:, g, :], pmT[:, tg, 0:n], True, True, [B("pwb"), RB("pmT", tg)], [pb2])
                p2.append((pt2, pb2))
            for tg in range(3):
                t0, n, gg = TGS[tg]
                act(obT[:, g, t0:t0 + n], p2[tg][0][:, 0:n], AF.Copy, [p2[tg][1]] + SM, [B("obT", g, tg)],
                    scale=psc[:, g:g + 1])

        new_phase()
        for n_ in range(8):
            k = WB_ML[n_]
            s = wget(k)
            wga = rslice(s, 0, (KC, 128))
            wgb = rslice(s, KC * 128, (KC, 128))
            wa = rslice(s, 2 * KC * 128, (4, 128))
            wb = rslice(s, 2 * KC * 128 + 512, (4, 128))
            for tg in range(3):
                t0, n, g = TGS[tg]
                pga, pgab = feat_major(lambda kc, w=wga: w[:, kc, :], wbufs(k), tg)
                pgb, pgbb = feat_major(lambda kc, w=wgb: w[:, kc, :], wbufs(k), tg)
                pA, pAb = ps()
                for e in range(4):
                    mm(pA[:, 0:n], wa[:, e, :], oaT[:, e, t0:t0 + n], e == 0, e == 3, wbufs(k) + [B("oaT", e, tg)], [pAb])
                pB, pBb = ps()
                for e in range(4):
                    mm(pB[:, 0:n], wb[:, e, :], obT[:, e, t0:t0 + n], e == 0, e == 3, wbufs(k) + [B("obT", e, tg)], [pBb])
                act(sg32[:, 0, 0:n], pga[:, 0:n], AF.Sigmoid, [pgab], [RB("sg32", 0)] + fz("act"))
                act(sg32[:, 1, 0:n], pgb[:, 0:n], AF.Sigmoid, [pgbb], [RB("sg32", 1)])
                tt("dve", m1[:, 0, 0:n], pA[:, 0:n], sg32[:, 0, 0:n], ALU.mult, [pAb, RB("sg32", 0)],
                   [RB("m1", 0)] + fz("dve"))
                tt("dve", m1[:, 1, 0:n], pB[:, 0:n], sg32[:, 1, 0:n], ALU.mult, [pBb, RB("sg32", 1)], [RB("m1", 1)])
                tt("dve", mergedT[:, n_, t0:t0 + n], m1[:, 0, 0:n], m1[:, 1, 0:n], ALU.add,
                   [RB("m1", 0), RB("m1", 1)], [RB("mergedT", n_, tg)])
                it_ = n_ * 3 + tg
                mini_consume(16 + it_)
                if it_ == 23:
                    make_G(G2, g2c, 32, "2")

        def resid_proj(kbase, nblk, src, sname, nk, gate_c0, kcs_per_blk):
            for b in range(nblk):
                k = kbase + b
                s = wget(k)
                wv = rslice(s, 0, (nk, kcs_per_blk * 128))
                for jj in range(kcs_per_blk):
                    n2 = b * kcs_per_blk + jj
                    for tg in range(3):
                        t0, n, g = TGS[tg]
                        pt, pb = ps()
                        for kc in range(nk):
                            mm(pt[:, 0:n], wv[:, kc, jj * 128:(jj + 1) * 128], src[:, kc, t0:t0 + n], kc == 0,
                               kc == nk - 1, wbufs(k) + [RB(sname, kc, tg)], [pb])
                        stt(xT[:, n2, t0:t0 + n], pt[:, 0:n], modT[:, gate_c0 + n2, g:g + 1], xT[:, n2, t0:t0 + n],
                            ALU.mult, ALU.add, [pb, B("xT", n2, tg), MB(gate_c0), MB(gate_c0 + 4)], [B("xT", n2, tg)])

        resid_proj(WB_O, 2, mergedT, "mergedT", KC, 16, 4)
        norm_to_hT(G2, "2", 24)

        new_phase()
        for b in range(11):
            k = WB_FL[b]
            s = wget(k)
            for jj in range(2):
                j = 2 * b + jj
                wg = rslice(s, (2 * jj) * KC * 128, (KC, 128))
                wu = rslice(s, (2 * jj + 1) * KC * 128, (KC, 128))
                for tg in range(3):
                    t0, n, g = TGS[tg]
                    pg, pgb_ = feat_major(lambda kc, w=wg: w[:, kc, :], wbufs(k), tg)
                    pu, pub_ = feat_major(lambda kc, w=wu: w[:, kc, :], wbufs(k), tg)
                    sl = (j * 3 + tg) % 2
                    act(ntmp[:, sl, 0:n], pg[:, 0:n], AF.Silu, [pgb_], [B("ntmp", sl)])
                    tt("dve", actT[:, j, t0:t0 + n], pu[:, 0:n], ntmp[:, sl, 0:n], ALU.mult,
                       [pub_, B("ntmp", sl)], [RB("actT", j, tg)] + fz("dve"))
                    it_ = j * 3 + tg
                    if it_ < 8:
                        mini_consume(40 + it_)

        resid_proj(WB_FO, 8, actT, "actT", FC, 40, 1)

        yb_first = set()
        sq_eng[0] = "act"
        for tg in range(3):
            t0, n, g = TGS[tg]
            rs = rms_rstd(tg)
            for kc in range(KC):
                stt(xT[:, kc, t0:t0 + n], xT[:, kc, t0:t0 + n], fgc[:, kc:kc + 1], rstd2[:, rs, 0:n], ALU.mult,
                    ALU.mult, [B("xT", kc, tg), B("rstd", rs)] + SM, [B("xT", kc, tg)])
            LFall = [B("LF", a, b) for a in range(2) for b in range(2)]
            for ii in range(n // 128):
                i = t0 // 128 + ii
                s4 = i % 4
                stg, sl = (ytok, s4) if s4 < 2 else (ytokB, s4 - 2)
                sb_ = YB[sl] if s4 < 2 else [B("ytokB", sl, 0), B("ytokB", sl, 1)]
                for hf in range(2):
                    pt, pb = ps()
                    for j in range(4):
                        kc = hf * 4 + j
                        tr(pt[:, j * 128:(j + 1) * 128], xT[:, kc, i * 128:(i + 1) * 128], ident32[:],
                           [B("xT", kc, tg)] + SM, [pb])
                    wr = [sb_[hf]]
                    if s4 >= 2 and (sl, hf) not in yb_first:
                        yb_first.add((sl, hf))
                        wr = wr + LFall
                    cp("act" if hf == 0 else "dve", stg[:, sl, hf * 512:(hf + 1) * 512], pt[:, :], [pb], wr)
                dma("sp", y_d[i * 128:(i + 1) * 128, :], stg[:, sl, :], sb_, [], ("yout", s4))

        if debug:
            dma("sp", dbg_oaT, oaT[:], [B("oaT", a, b) for a in range(4) for b in range(3)], [], ("dbg",))
            dma("sp", dbg_obT, obT[:], [B("obT", a, b) for a in range(4) for b in range(3)], [], ("dbg",))
        fk = [("dbg",), ("yout", 0), ("yout", 1), ("yout", 2), ("yout", 3), ("stout", 0), ("stout", 1)]
        P.emit(nc, stack, [k_ for k_ in fk if k_ in P.dma_counts])
    return nc


_CACHE = {}


def kernel(x_prompt, x_sample, state_hgrn, c, c_ctx, w_ada, b_ada, norm1_g, w_in, hgrn_lb_logits, hgrn_norm_g,
           w_branch_a, pool_w, pool_scale, w_branch_b, w_out, norm2_g, w_ffn_in, w_ffn_out, final_g):
    f32 = lambda a: np.ascontiguousarray(np.asarray(a, dtype=np.float32))
    x_prompt, x_sample, state_hgrn, c, c_ctx = map(f32, (x_prompt, x_sample, state_hgrn, c, c_ctx))
    import os
    dbg = bool(os.environ.get("KDEBUG"))
    if "nc" not in _CACHE:
        _CACHE["nc"] = build_nc(debug=dbg)
        _CACHE["pc"] = _pool_consts()
        _CACHE["sc"] = _scan_consts()
    nc = _CACHE["nc"]
    p1, i1, p2, i2 = _CACHE["pc"]
    mf, mb, rmask = _CACHE["sc"]
    bf = ml_dtypes.bfloat16

    p1T = np.ascontiguousarray(p1.transpose(0, 2, 1))
    p2T = np.ascontiguousarray(p2.transpose(0, 2, 1))
    poolA_prompt = np.zeros((4, 1024, 1024), np.float32)
    for q in range(4):
        poolA_prompt[:, q * 256:(q + 1) * 256, q * 256:(q + 1) * 256] = p1T
    poolA_prompt = poolA_prompt.astype(bf)
    poolA_sample = p2T.astype(bf)
    poolB = p1T.astype(bf)
    inv_prompt = np.concatenate([np.tile(i1, (1, 4)), i1], axis=1)
    inv_sample = np.concatenate([i2, i1], axis=1)

    lbl = f32(hgrn_lb_logits)
    shared = {
        "mf": mf.astype(bf), "mb": mb.astype(bf), "rmask": rmask, "id32": np.eye(128, dtype=np.float32),
        "w_ada": f32(w_ada[0]), "b_adaT": _colT(b_ada[0], 48), "n1gT": _colT(norm1_g[0], 8),
        "n2gT": _colT(norm2_g[0], 8), "fgT": _colT(final_g, 8), "w_in": f32(w_in[0]),
        "lb0T": _colT(lbl[0].reshape(-1), 8), "lb1T": _colT(lbl[1].reshape(-1), 8),
        "hngT": _colT(hgrn_norm_g[0], 4), "wba": f32(w_branch_a[0]), "poolw": f32(pool_w[0]),
        "pscT": _colT(pool_scale[0], 4), "wbb": f32(w_branch_b[0]), "wout": f32(w_out[0]),
        "wffin": f32(w_ffn_in[0]), "wffout": f32(w_ffn_out[0]), "poolB": poolB,
    }
    in_maps = []
    plan = []
    for core in range(N_CORES):
        if core < 2:
            segs = [("s", core, q) for q in range(4)] + [("p", core, 0)]
            x = np.concatenate([x_sample[core], x_prompt[core]], axis=0)
            cv = np.stack([c[core], c_ctx])
            s0 = np.zeros((5, 2, 4, 128, 128), np.float32)
            s0[0, 0] = state_hgrn[core, 0, 0]
            s0[3, 1] = state_hgrn[core, 0, 1]
            flags = np.zeros((128, 10), np.float32)
            flags[:, 1:4] = 1.0
            flags[:, 5:8] = 1.0
            pa, inv = poolA_sample, inv_sample
        else:
            ids = [2 + (core - 2) * 5 + q for q in range(5)]
            segs = [("p", i, 0) for i in ids]
            x = np.concatenate([x_prompt[i] for i in ids], axis=0)
            cv = np.stack([c_ctx, c_ctx])
            s0 = np.zeros((5, 2, 4, 128, 128), np.float32)
            flags = np.zeros((128, 10), np.float32)
            pa, inv = poolA_prompt, inv_prompt
        plan.append(segs)
        cT = np.ascontiguousarray(cv.reshape(2, 8, 128).transpose(2, 0, 1).reshape(128, 16))
        m = dict(shared)
        m.update({"x": np.ascontiguousarray(x), "cT": cT, "s0": s0, "flags": flags, "poolA": pa,
                  "invc": np.ascontiguousarray(inv.astype(np.float32))})
        in_maps.append(m)

    res = run_bass_kernel_spmd(nc, in_maps, core_ids=list(range(N_CORES)))
    if dbg:
        _CACHE["dbg"] = [{k_: np.asarray(v_) for k_, v_ in r.items()} for r in res.results]
    y_prompt = np.zeros((32, 256, D), np.float32)
    y_sample = np.zeros((2, 1024, D), np.float32)
    new_state = np.zeros((32, 1, 2, 4, 128, 128), np.float32)
    for core in range(N_CORES):
        y = np.asarray(res.results[core]["y"], dtype=np.float32)
        st = np.asarray(res.results[core]["st"], dtype=np.float32)
        for sgi, (kind, idx, q) in enumerate(plan[core]):
            ys = y[sgi * 256:(sgi + 1) * 256]
            if kind == "s":
                y_sample[idx, q * 256:(q + 1) * 256] = ys
            else:
                y_prompt[idx] = ys
                new_state[idx, 0] = st[sgi]
    return (y_prompt, y_sample, new_state)
```
